# Optimizing a Trainium2 kernel written in Bass

```python
import jax, jax.numpy as jnp
from jax import lax
import numpy as np

D_MODEL = 1024
BATCH = 32
SEQ = 2048
DEPTH = 1

N_META = 16
GRID_W = 64
HEAD_DIM = 128
N_Q_HEADS = 8
N_KV_HEADS = 2
Q_GROUP = N_Q_HEADS // N_KV_HEADS
Q_WIDTH = N_Q_HEADS * HEAD_DIM
KV_WIDTH = N_KV_HEADS * HEAD_DIM
RNN_WIDTH = D_MODEL
RNN_BLOCKS = 8
RNN_BLOCK = RNN_WIDTH // RNN_BLOCKS
CONV_W = 4
CONV_PAD_L = CONV_W // 2
CONV_PAD_R = CONV_W - 1 - CONV_PAD_L
RG_C = 8.0
ROPE_THETA = 10000.0
ROPE_AXIS_DIM = HEAD_DIM // 2
ROPE_PAIRS = ROPE_AXIS_DIM // 2
Q_BLOCK = 128
D_FF = ((8 * D_MODEL + 3 * 256 - 1) // (3 * 256)) * 256
IN_WIDTH = Q_WIDTH + 2 * KV_WIDTH + 2 * RNN_WIDTH + 2 * D_MODEL
EPS = 1e-6

kernel_name = 'hybrid_rglru_axial_gqa_encoder_block'


def rmsnorm(x, g):
    xf = x.astype(jnp.float32)
    y = xf * lax.rsqrt(jnp.mean(xf * xf, axis=-1, keepdims=True) + EPS)
    return (y * g.astype(jnp.float32)).astype(x.dtype)


def axial_rope_tables(n_tok):
    rows = n_tok // GRID_W
    row = jnp.repeat(jnp.arange(rows, dtype=jnp.float32), GRID_W)
    col = jnp.tile(jnp.arange(GRID_W, dtype=jnp.float32), rows)
    zeros = jnp.zeros((N_META,), jnp.float32)
    row = jnp.concatenate([zeros, row])
    col = jnp.concatenate([zeros, col])
    inv_freq = jnp.exp(-jnp.log(jnp.float32(ROPE_THETA)) * jnp.arange(ROPE_PAIRS, dtype=jnp.float32) / ROPE_PAIRS)
    ang_r = row[:, None] * inv_freq[None, :]
    ang_c = col[:, None] * inv_freq[None, :]
    return jnp.cos(ang_r), jnp.sin(ang_r), jnp.cos(ang_c), jnp.sin(ang_c)


def rotate_half_axis(xh, cos, sin):
    c = cos[None, :, None, :]
    s = sin[None, :, None, :]
    x1, x2 = xh[..., :ROPE_PAIRS], xh[..., ROPE_PAIRS:]
    return jnp.concatenate([x1 * c - x2 * s, x2 * c + x1 * s], axis=-1)


def apply_axial_rope(x, tabs):
    cos_r, sin_r, cos_c, sin_c = tabs
    xf = x.astype(jnp.float32)
    xr = rotate_half_axis(xf[..., :ROPE_AXIS_DIM], cos_r, sin_r)
    xc = rotate_half_axis(xf[..., ROPE_AXIS_DIM:], cos_c, sin_c)
    return jnp.concatenate([xr, xc], axis=-1).astype(x.dtype)


def bidirectional_gqa(q, k, v):
    B, T = q.shape[0], q.shape[1]
    q = q.reshape(B, T, N_KV_HEADS, Q_GROUP, HEAD_DIM)
    scale = HEAD_DIM ** -0.5

    def block(qb):
        s = jnp.einsum('bqhgd,bkhd->bhgqk', qb, k, preferred_element_type=jnp.float32) * scale
        p = jax.nn.softmax(s, axis=-1)
        return jnp.einsum('bhgqk,bkhd->bqhgd', p.astype(v.dtype), v)

    q_meta, q_real = q[:, :N_META], q[:, N_META:]
    n_blk = q_real.shape[1] // Q_BLOCK
    qr = q_real.reshape(B, n_blk, Q_BLOCK, N_KV_HEADS, Q_GROUP, HEAD_DIM).transpose(1, 0, 2, 3, 4, 5)
    o_real = lax.map(block, qr)
    o_real = o_real.transpose(1, 0, 2, 3, 4, 5).reshape(B, n_blk * Q_BLOCK, Q_WIDTH)
    o_meta = block(q_meta).reshape(B, N_META, Q_WIDTH)
    return jnp.concatenate([o_meta, o_real], axis=1)


def centred_dwconv(x, w, b):
    T = x.shape[1]
    xp = jnp.pad(x, ((0, 0), (CONV_PAD_L, CONV_PAD_R), (0, 0)))
    y = b
    for j in range(CONV_W):
        y = y + xp[:, j:j + T] * w[j]
    return y


def _linear_recurrence_combine(e1, e2):
    a1, b1 = e1
    a2, b2 = e2
    return a1 * a2, a2 * b1 + b2


def rg_lru(xc, wa, ba, wx, bx, lam, reverse):
    B, T, C = xc.shape
    xf = xc.astype(jnp.float32)
    xb = xf.reshape(B, T, RNN_BLOCKS, RNN_BLOCK)
    r = jax.nn.sigmoid(jnp.einsum('btnc,ncd->btnd', xb, wa.astype(jnp.float32)).reshape(B, T, C) + ba.astype(jnp.float32))
    i = jax.nn.sigmoid(jnp.einsum('btnc,ncd->btnd', xb, wx.astype(jnp.float32)).reshape(B, T, C) + bx.astype(jnp.float32))
    log_a = RG_C * r * jax.nn.log_sigmoid(lam.astype(jnp.float32))
    a = jnp.exp(log_a)
    u = jnp.sqrt(-jnp.expm1(2.0 * log_a)) * (i * xf)
    _, h = lax.associative_scan(_linear_recurrence_combine, (a, u), axis=1, reverse=reverse)
    return h


def setup_inputs(seed: int = 0) -> dict:
    key = jax.random.key(seed)
    ks = jax.random.split(key, 20)
    f32 = jnp.float32

    def nrm(k, shape, fan_in):
        return jax.random.normal(k, shape, f32) * (fan_in ** -0.5)

    x = jax.random.normal(ks[0], (BATCH, SEQ, D_MODEL), f32)
    meta_tokens = jax.random.normal(ks[1], (N_META, D_MODEL), f32)
    norm1_g = 1.0 + 0.05 * jax.random.normal(ks[2], (DEPTH, D_MODEL), f32)
    w_in = nrm(ks[3], (DEPTH, D_MODEL, IN_WIDTH), D_MODEL)
    conv_w = nrm(ks[4], (DEPTH, CONV_W, RNN_WIDTH), CONV_W)
    conv_b = 0.02 * jax.random.normal(ks[5], (DEPTH, RNN_WIDTH), f32)
    rg_wa = nrm(ks[6], (DEPTH, 2, RNN_BLOCKS, RNN_BLOCK, RNN_BLOCK), RNN_BLOCK)
    rg_ba = 0.02 * jax.random.normal(ks[7], (DEPTH, 2, RNN_WIDTH), f32)
    rg_wx = nrm(ks[8], (DEPTH, 2, RNN_BLOCKS, RNN_BLOCK, RNN_BLOCK), RNN_BLOCK)
    rg_bx = 0.02 * jax.random.normal(ks[9], (DEPTH, 2, RNN_WIDTH), f32)
    a_c = jax.random.uniform(ks[10], (DEPTH, 2, RNN_WIDTH), f32, 0.9, 0.999)
    s = a_c ** (1.0 / RG_C)
    rg_lambda = jnp.log(s) - jnp.log1p(-s)
    q_norm_g = 1.0 + 0.05 * jax.random.normal(ks[11], (DEPTH, HEAD_DIM), f32)
    k_norm_g = 1.0 + 0.05 * jax.random.normal(ks[12], (DEPTH, HEAD_DIM), f32)
    w_out = nrm(ks[13], (DEPTH, D_MODEL, D_MODEL), D_MODEL)
    norm2_g = 1.0 + 0.05 * jax.random.normal(ks[14], (DEPTH, D_MODEL), f32)
    w_ffn_in = nrm(ks[15], (DEPTH, D_MODEL, 2 * D_FF), D_MODEL)
    w_ffn_out = nrm(ks[16], (DEPTH, D_FF, D_MODEL), D_FF)
    return {'x': x, 'meta_tokens': meta_tokens, 'norm1_g': norm1_g, 'w_in': w_in,
            'conv_w': conv_w, 'conv_b': conv_b, 'rg_wa': rg_wa, 'rg_ba': rg_ba,
            'rg_wx': rg_wx, 'rg_bx': rg_bx, 'rg_lambda': rg_lambda,
            'q_norm_g': q_norm_g, 'k_norm_g': k_norm_g, 'w_out': w_out,
            'norm2_g': norm2_g, 'w_ffn_in': w_ffn_in, 'w_ffn_out': w_ffn_out}


def reference(x, meta_tokens, norm1_g, w_in, conv_w, conv_b, rg_wa, rg_ba, rg_wx, rg_bx,
              rg_lambda, q_norm_g, k_norm_g, w_out, norm2_g, w_ffn_in, w_ffn_out):
    B, S, D = x.shape
    meta = jnp.broadcast_to(meta_tokens.astype(x.dtype)[None], (B, N_META, D))
    h = jnp.concatenate([meta, x], axis=1)
    T = h.shape[1]
    tabs = axial_rope_tables(S)
    splits = [Q_WIDTH, Q_WIDTH + KV_WIDTH, Q_WIDTH + 2 * KV_WIDTH,
              Q_WIDTH + 2 * KV_WIDTH + RNN_WIDTH, Q_WIDTH + 2 * KV_WIDTH + 2 * RNN_WIDTH]

    for l in range(DEPTH):
        xn = rmsnorm(h, norm1_g[l])
        proj = xn @ w_in[l]
        q, k, v, xr, gr, g_merge = jnp.split(proj, splits, axis=-1)

        q = q.reshape(B, T, N_Q_HEADS, HEAD_DIM)
        k = k.reshape(B, T, N_KV_HEADS, HEAD_DIM)
        v = v.reshape(B, T, N_KV_HEADS, HEAD_DIM)
        q = apply_axial_rope(rmsnorm(q, q_norm_g[l]), tabs)
        k = apply_axial_rope(rmsnorm(k, k_norm_g[l]), tabs)
        attn = bidirectional_gqa(q, k, v)

        xc = centred_dwconv(xr, conv_w[l], conv_b[l])
        rnn = (rg_lru(xc, rg_wa[l, 0], rg_ba[l, 0], rg_wx[l, 0], rg_bx[l, 0], rg_lambda[l, 0], False)
               + rg_lru(xc, rg_wa[l, 1], rg_ba[l, 1], rg_wx[l, 1], rg_bx[l, 1], rg_lambda[l, 1], True))
        rnn = rnn.astype(x.dtype) * jax.nn.gelu(gr)

        gates = jax.nn.sigmoid(g_merge)
        g_attn, g_rnn = gates[..., :D_MODEL], gates[..., D_MODEL:]
        mix = g_attn * attn + g_rnn * rnn
        h = h + mix @ w_out[l]

        hn = rmsnorm(h, norm2_g[l])
        gu = hn @ w_ffn_in[l]
        g, u = gu[..., :D_FF], gu[..., D_FF:]
        h = h + (jax.nn.silu(g) * u) @ w_ffn_out[l]

    return h[:, N_META:]
```

```python
import types
import numpy as np
import concourse.bass as bass
import concourse.mybir as mybir
from concourse.bass_utils import run_bass_kernel_spmd

F32 = mybir.dt.float32
BF16 = mybir.dt.bfloat16
ALU = mybir.AluOpType
AF = mybir.ActivationFunctionType
AX = mybir.AxisListType

N_CORES = 8
BATCH, SEQ, D = 32, 2048, 1024
NB_CORE = BATCH // N_CORES
N_META = 16
T = SEQ + N_META
NT_TILES = SEQ // 128
HD = 128
NQ, NKV = 8, 2
D_FF = 2816
NFF = D_FF // 128
IN_W = 5632
EPS = 1e-6
C_Q, C_K, C_V, C_XR, C_GR, C_GA, C_GN = 0, 1024, 1280, 1536, 2560, 3584, 4608
WIN = 509
NWIN = 5
GK = 0.7978845608028654
SM_SCALE = HD ** -0.5


class Buf:
    __slots__ = ("name", "w", "r")

    def __init__(self, name):
        self.name = name
        self.w = None
        self.r = []


class Op:
    __slots__ = ("eng", "fn", "deps", "sig", "sem", "val", "dma")


def _freeze(fn):
    if fn is None or fn.__closure__ is None:
        return fn
    cells = []
    for c in fn.__closure__:
        try:
            cells.append(types.CellType(c.cell_contents))
        except ValueError:
            cells.append(c)
    return types.FunctionType(fn.__code__, fn.__globals__, fn.__name__, fn.__defaults__, tuple(cells))


ENGS = ("pe", "act", "dve", "pool", "sp")
N_DMA_SEMS = 40


class Sched:
    def __init__(self):
        self.ops = {e: [] for e in ENGS}
        self.dma_ops = []

    def op(self, eng, fn, reads=(), writes=(), dma=False, extra=()):
        o = Op()
        o.eng, o.fn, o.dma, o.sig, o.sem, o.val = eng, _freeze(fn), dma, False, None, 0
        deps = list(extra)
        for b in reads:
            if b.w is not None:
                deps.append(b.w)
        for b in writes:
            if b.w is not None:
                deps.append(b.w)
            deps.extend(b.r)
        seen, out = set(), []
        for d in deps:
            if id(d) in seen or d is o:
                continue
            seen.add(id(d))
            if d.eng == "pe" and eng == "pe" and not d.dma:
                continue
            out.append(d)
            d.sig = True
        o.deps = out
        for b in reads:
            b.r.append(o)
        for b in writes:
            b.w = o
            b.r = []
        self.ops[eng].append(o)
        if dma:
            self.dma_ops.append(o)
        return o

    def barrier(self):
        last = []
        for e in ENGS:
            for o in reversed(self.ops[e]):
                if o.fn is not None and not o.dma:
                    last.append(o)
                    break
        dmas = [o for o in self.dma_ops if o.eng == "sp"][-N_DMA_SEMS // 2:] + \
               [o for o in self.dma_ops if o.eng == "pool"][-N_DMA_SEMS // 2:]
        for e in ENGS:
            self.op(e, None, extra=last + dmas)

    def emit(self, nc, block, sems, dsems):
        for e in ENGS:
            cnt = 0
            for o in self.ops[e]:
                if o.dma or not o.sig:
                    continue
                cnt += 1
                o.sem, o.val = sems[e], cnt
        half = len(dsems) // 2
        pools = {"sp": dsems[:half], "pool": dsems[half:]}
        use = {id(x): 0 for x in dsems}
        prev = {id(x): None for x in dsems}
        cnt = {"sp": 0, "pool": 0}
        for o in self.dma_ops:
            pl = pools[o.eng]
            sm = pl[cnt[o.eng] % len(pl)]
            cnt[o.eng] += 1
            use[id(sm)] += 1
            o.sem, o.val = sm, 16 * use[id(sm)]
            if prev[id(sm)] is not None:
                o.deps.append(prev[id(sm)])
            prev[id(sm)] = o

        def run(eng_name, eng):
            waited = {}
            for o in self.ops[eng_name]:
                for d in o.deps:
                    key = id(d.sem)
                    if waited.get(key, 0) < d.val:
                        eng.wait_ge(d.sem, d.val)
                        waited[key] = d.val
                if o.fn is None:
                    continue
                inst = o.fn(eng)
                if o.dma:
                    inst.then_inc(o.sem, 16)
                elif o.sig:
                    inst.then_inc(o.sem, 1)

        @block.tensor
        def _(e):
            run("pe", e)

        @block.scalar
        def _(e):
            run("act", e)

        @block.vector
        def _(e):
            run("dve", e)

        @block.gpsimd
        def _(e):
            run("pool", e)

        @block.sync
        def _(e):
            run("sp", e)


def rope_tables():
    rows = SEQ // 64
    row = np.repeat(np.arange(rows, dtype=np.float32), 64)
    col = np.tile(np.arange(64, dtype=np.float32), rows)
    z = np.zeros((N_META,), np.float32)
    row = np.concatenate([z, row])
    col = np.concatenate([z, col])
    inv = np.exp(-np.log(np.float32(10000.0)) * np.arange(32, dtype=np.float32) / np.float32(32)).astype(np.float32)
    ar = (row[:, None] * inv[None, :]).astype(np.float32)
    ac = (col[:, None] * inv[None, :]).astype(np.float32)
    cos = np.concatenate([np.cos(ar), np.cos(ar), np.cos(ac), np.cos(ac)], axis=1).astype(np.float32)
    sin = np.concatenate([-np.sin(ar), np.sin(ar), -np.sin(ac), np.sin(ac)], axis=1).astype(np.float32)
    return np.ascontiguousarray(cos), np.ascontiguousarray(sin)


def build(nb=NB_CORE, debug=False):
    nc = bass.Bass("TRN2", target_bir_lowering=False)
    S = Sched()

    def din(name, shape):
        return nc.dram_tensor(name, list(shape), F32, kind="ExternalInput").ap()

    x_d = din("x", [NB_CORE, SEQ, D])
    meta_d = din("meta_tokens", [N_META, D])
    n1_d = din("norm1_g", [1, D])
    win_d = din("w_in", [1, D, IN_W])
    cw_d = din("conv_w", [1, 4, D])
    cb_d = din("conv_b", [1, D])
    wa_d = din("rg_wa", [1, 2, 8, 128, 128])
    ba_d = din("rg_ba", [1, 2, D])
    wx_d = din("rg_wx", [1, 2, 8, 128, 128])
    bx_d = din("rg_bx", [1, 2, D])
    lam_d = din("rg_lambda", [1, 2, D])
    qg_d = din("q_norm_g", [1, HD])
    kg_d = din("k_norm_g", [1, HD])
    wo_d = din("w_out", [1, D, D])
    n2_d = din("norm2_g", [1, D])
    w1_d = din("w_ffn_in", [1, D, 2 * D_FF])
    w2_d = din("w_ffn_out", [1, D_FF, D])
    cos_d = din("rope_cos", [T, HD])
    sin_d = din("rope_sin", [T, HD])
    out_d = nc.dram_tensor("out", [NB_CORE, SEQ, D], F32, kind="ExternalOutput").ap()
    hbuf_d = (nc.dram_tensor("hbuf", [NB_CORE * SEQ, D], F32, kind="ExternalOutput").ap() if debug
              else nc.dram_tensor("hbuf", [NB_CORE * SEQ, D], F32).ap())
    dbg = {}
    if debug:
        for nm, shp in (("d_xnT", [128, 8 * (T + 4)]), ("d_KT", [128, 2 * T]), ("d_V", [128, 17 * 256]),
                        ("d_mixr", [128, 8 * SEQ])):
            dbg[nm] = nc.dram_tensor(nm, shp, F32, kind="ExternalOutput").ap()

    win_v = win_d[0].rearrange("(k p) n -> p k n", p=128)
    wo_v = wo_d[0].rearrange("(k p) n -> p k n", p=128)
    w1_v = w1_d[0].rearrange("(k p) n -> p k n", p=128)
    w2_v = w2_d[0].rearrange("(k p) n -> p k n", p=128)

    ARENA_W = 53184
    arena_cm = nc.sbuf_tensor("arena", [128, ARENA_W], F32)
    psum_cm = nc.psum_tensor("ps", [128, 8, 512], F32)
    arena = arena_cm.__enter__()
    psum = psum_cm.__enter__()

    class Alloc:
        def __init__(self):
            self.ptr = 0
            self.marks = []

        def f32(self, shape, name):
            n = int(np.prod(shape[1:]))
            v = arena[:, self.ptr:self.ptr + n]
            self.ptr += n
            assert self.ptr <= ARENA_W, (name, self.ptr)
            if len(shape) == 3:
                v = v.rearrange("p (a b) -> p a b", a=shape[1])
            return v

        def bf16(self, shape, name):
            n = int(np.prod(shape[1:]))
            assert n % 2 == 0
            v = arena[:, self.ptr:self.ptr + n // 2].bitcast(BF16)
            self.ptr += n // 2
            assert self.ptr <= ARENA_W, (name, self.ptr)
            if len(shape) == 3:
                v = v.rearrange("p (a b) -> p a b", a=shape[1])
            return v

        def mark(self):
            self.marks.append(self.ptr)

        def release(self):
            self.ptr = self.marks.pop()

    A = Alloc()

    def psf(b, n=512):
        return psum[:, b, 0:n]

    def psb(b):
        return psum[:, b, 0:256].bitcast(BF16).rearrange("p (a b) -> p a b", a=4)

    PB = [Buf("psum%d" % i) for i in range(8)]

    ident = A.bf16([128, 128], "ident")
    ones = A.bf16([128, 128], "ones")
    g1b = A.f32([128, D], "g1b")
    gq = A.f32([128, HD], "gq")
    gqs = A.f32([128, HD], "gqs")
    gk = A.f32([128, HD], "gk")
    gks = A.f32([128, HD], "gks")
    cw = A.f32([128, 4, 8], "cw")
    cb = A.f32([128, 8], "cb")
    bah = A.f32([128, 16], "bah")
    bxh = A.f32([128, 16], "bxh")
    lam = A.f32([128, 16], "lam")
    cc = A.f32([128, 16], "cc")
    cch = A.f32([128, 16], "cch")
    tmpc = A.f32([128, 16], "tmpc")
    tmpe = A.f32([128, 16], "tmpe")
    rstd2_all = A.f32([128, NB_CORE * NT_TILES], "rstd2_all")
    rstd1_all = A.f32([128, 32], "rstd1_all")
    B_rstd1 = [Buf("rstd1_%d" % i) for i in range(17)]
    wab = A.bf16([128, 16 * 128], "wab").rearrange("p (a b) -> p a b", a=16)
    wxb = A.bf16([128, 16 * 128], "wxb").rearrange("p (a b) -> p a b", a=16)
    B_const = Buf("const")
    B_rstd2 = [Buf("rstd2_%d" % i) for i in range(NB_CORE * NT_TILES)]
    B_hbuf = [Buf("hbuf_%d" % i) for i in range(NB_CORE * NT_TILES)]

    def ld(eng, out, in_, wbufs, rbufs=(), **kw):
        return S.op(eng, lambda e, out=out, in_=in_, kw=kw: e.dma_start(out=out, in_=in_, **kw), reads=rbufs, writes=wbufs,
                    dma=True)

    def bc(ap, n):
        return bass.AP(ap.tensor, ap.offset, [[0, 128], [1, n]])

    S.op("pool", lambda e: e.memset(ones, 1.0), writes=[B_const])
    ident_d = din("ident", [128, 128])
    identf = A.f32([128, 128], "identf")
    ld("sp", identf, ident_d, [B_const])
    S.op("dve", lambda e: e.tensor_copy(out=ident, in_=identf), reads=[B_const], writes=[B_const])
    ld("sp", g1b, bc(n1_d[0:1, :], D), [B_const])
    ld("sp", gq, bc(qg_d[0:1, :], HD), [B_const])
    ld("sp", gk, bc(kg_d[0:1, :], HD), [B_const])
    for a in range(2):
        for h in range(2):
            o0 = 64 * a + 32 * h
            s0 = 64 * a + 32 * (1 - h)
            ld("sp", gqs[:, o0:o0 + 32], bc(qg_d[0:1, s0:s0 + 32], 32), [B_const])
            ld("sp", gks[:, o0:o0 + 32], bc(kg_d[0:1, s0:s0 + 32], 32), [B_const])
    ld("sp", cw, cw_d[0].rearrange("j (c p) -> p j c", p=128), [B_const], allow_slow_non_contiguous=True)
    ld("sp", cb, cb_d[0].rearrange("(c p) -> p c", p=128), [B_const], allow_slow_non_contiguous=True)
    ld("sp", bah.rearrange("p (r c) -> p r c", r=2), ba_d[0].rearrange("r (c p) -> p r c", p=128), [B_const],
       allow_slow_non_contiguous=True)
    ld("sp", bxh.rearrange("p (r c) -> p r c", r=2), bx_d[0].rearrange("r (c p) -> p r c", p=128), [B_const],
       allow_slow_non_contiguous=True)
    ld("sp", lam.rearrange("p (r c) -> p r c", r=2), lam_d[0].rearrange("r (c p) -> p r c", p=128), [B_const],
       allow_slow_non_contiguous=True)
    ld("pool", wab.rearrange("p (r n) d -> p r n d", r=2), wa_d[0].rearrange("r n c d -> c r n d"), [B_const])
    ld("pool", wxb.rearrange("p (r n) d -> p r n d", r=2), wx_d[0].rearrange("r n c d -> c r n d"), [B_const])
    S.op("dve", lambda e: e.tensor_scalar(out=bah, in0=bah, scalar1=0.5, scalar2=None, op0=ALU.mult),
         reads=[B_const], writes=[B_const])
    S.op("dve", lambda e: e.tensor_scalar(out=bxh, in0=bxh, scalar1=0.5, scalar2=None, op0=ALU.mult),
         reads=[B_const], writes=[B_const])
    S.op("act", lambda e: e.activation(out=tmpe, in_=lam, func=AF.Exp, scale=-1.0), reads=[B_const], writes=[B_const])
    S.op("dve", lambda e: e.tensor_scalar(out=tmpc, in0=tmpe, scalar1=-0.25, scalar2=1.0 / 3.0, op0=ALU.mult,
                                          op1=ALU.add), reads=[B_const], writes=[B_const])
    S.op("dve", lambda e: e.tensor_tensor(out=tmpc, in0=tmpc, in1=tmpe, op=ALU.mult), reads=[B_const], writes=[B_const])
    S.op("dve", lambda e: e.scalar_tensor_tensor(out=tmpc, in0=tmpc, scalar=-0.5, in1=tmpe, op0=ALU.add, op1=ALU.mult),
         reads=[B_const], writes=[B_const])
    S.op("dve", lambda e: e.scalar_tensor_tensor(out=tmpc, in0=tmpc, scalar=1.0, in1=tmpe, op0=ALU.add, op1=ALU.mult),
         reads=[B_const], writes=[B_const])
    S.op("dve", lambda e: e.tensor_scalar(out=cc, in0=tmpc, scalar1=-8.0, scalar2=None, op0=ALU.mult),
         reads=[B_const], writes=[B_const])
    S.op("dve", lambda e: e.tensor_scalar(out=cch, in0=tmpc, scalar1=-4.0, scalar2=None, op0=ALU.mult),
         reads=[B_const], writes=[B_const])

    A.mark()
    wq = A.bf16([128, 8, 1024], "wq")
    wga = A.bf16([128, 8, 1024], "wga")
    wo = A.bf16([128, 8, 1024], "wo")
    B_wq, B_wga, B_wo = Buf("wq"), Buf("wga"), Buf("wo")
    for k in range(8):
        ld("pool", wq[:, k, :], win_v[:, k, C_Q:C_Q + 1024], [B_wq])
    for k in range(8):
        ld("pool", wga[:, k, :], win_v[:, k, C_GA:C_GA + 1024], [B_wga])
    for k in range(8):
        ld("pool", wo[:, k, :], wo_v[:, k, :], [B_wo])

    XNT0 = A.ptr
    xnT = A.bf16([128, 8, T + 4], "xnT")
    XNT1 = A.ptr
    mixr = A.bf16([128, 8, SEQ], "mixr")
    B_xnT = [Buf("xnT%d" % i) for i in range(17)]
    B_xpad = Buf("xnTpad")
    B_mixr = [[Buf("mixr%d_%d" % (c, i)) for i in range(NT_TILES)] for c in range(8)]
    B_KT = [Buf("KT%d" % i) for i in range(17)]
    B_V = [Buf("V%d" % i) for i in range(17)]

    def tile_cols(i):
        return (0, 16) if i == 0 else (16 + (i - 1) * 128, 128)

    def tiles_overlapping(p0, p1):
        res = []
        for i in range(17):
            s, n = tile_cols(i)
            if s < p1 and s + n > p0:
                res.append(i)
        return res

    WORK0 = A.ptr

    def rsqrt_small(ms, out, n, width, Bms, Bout):
        S.op("act", lambda e: e.activation(out=ms[:n, :width], in_=ms[:n, :width], func=AF.Ln), reads=[Bms], writes=[Bms])
        S.op("act", lambda e: e.activation(out=out[:n, :width], in_=ms[:n, :width], func=AF.Exp, scale=-0.5),
             reads=[Bms], writes=[Bout])

    class TileWork:
        pass

    def alloc_tilework(nx=2, with_q=True):
        W = TileWork()
        W.xt = [A.f32([128, D], "xt") for _ in range(nx)]
        W.Bxt = [Buf("xt%d" % i) for i in range(nx)]
        W.junk = A.bf16([128, D], "junk")
        W.Bjunk = Buf("junk")
        W.ss = [A.f32([128, 16], "ss") for _ in range(2)]
        W.Bss = [Buf("ss%d" % i) for i in range(2)]
        W.rs = [A.f32([128, 16], "rs") for _ in range(2)]
        W.Brs = [Buf("rs%d" % i) for i in range(2)]
        W.xn = [A.bf16([128, D], "xn") for _ in range(2)]
        W.Bxn = [Buf("xn%d" % i) for i in range(2)]
        W.cos = [A.f32([128, HD], "cos") for _ in range(2)]
        W.sin = [A.f32([128, HD], "sin") for _ in range(2)]
        W.Btab = [Buf("tab%d" % i) for i in range(2)]
        W.t1 = A.f32([128, HD], "t1")
        W.t2 = A.f32([128, HD], "t2")
        W.Bt12 = Buf("t12")
        nq = 1024 if with_q else 256
        W.qn = A.f32([128, nq], "qn")
        W.Bqn = Buf("qn")
        W.m1 = A.f32([128, nq], "m1")
        W.Bm1 = Buf("m1")
        W.m2 = A.f32([128, nq], "m2")
        W.Bm2 = Buf("m2")
        W.sq, W.Bsq = W.m2, W.Bm2
        W.qr = A.bf16([128, nq], "qr")
        W.Bqr = Buf("qr")
        W.cnt = 0
        return W

    def load_x(W, slot, b, i):
        s, n = tile_cols(i)
        src = meta_d[0:16, :] if i == 0 else x_d[b, (i - 1) * 128:i * 128, :]
        ld("sp", W.xt[slot][:n, :], src, [W.Bxt[slot]])

    def norm1_tile(W, slot, n, dst_xnT, dst_bufs, banks, stats=None, reuse=None, delay=0):
        r = W.cnt % 2
        W.cnt += 1
        xt, ss, rs, xn = W.xt[slot], W.ss[r], W.rs[r], W.xn[r]
        if reuse is None:
            S.op("act", lambda e: e.activation(out=W.junk[:n, :], in_=xt[:n, :], func=AF.Square, accum_out=ss[:n, 0:1]),
                 reads=[W.Bxt[slot]], writes=[W.Bjunk, W.Bss[r]])
            yield
            S.op("dve", lambda e: e.tensor_scalar(out=ss[:n, 0:1], in0=ss[:n, 0:1], scalar1=1.0 / D, scalar2=EPS,
                                                  op0=ALU.mult, op1=ALU.add), reads=[W.Bss[r]], writes=[W.Bss[r]])
            yield
            if stats is not None:
                rs_ap, rs_buf = stats
                S.op("act", lambda e: e.activation(out=ss[:n, 0:1], in_=ss[:n, 0:1], func=AF.Ln), reads=[W.Bss[r]], writes=[W.Bss[r]])
                S.op("act", lambda e: e.activation(out=rs_ap[:n, :], in_=ss[:n, 0:1], func=AF.Exp, scale=-0.5),
                     reads=[W.Bss[r]], writes=[rs_buf])
            else:
                rs_ap, rs_buf = rs[:, 0:1], W.Brs[r]
                rsqrt_small(ss, rs, n, 1, W.Bss[r], W.Brs[r])
            yield
        else:
            rs_ap, rs_buf = reuse
        S.op("dve", lambda e: e.scalar_tensor_tensor(out=xn[:n, :], in0=xt[:n, :], scalar=rs_ap[:n, :], in1=g1b[:n, :],
                                                     op0=ALU.mult, op1=ALU.mult),
             reads=[W.Bxt[slot], rs_buf, B_const], writes=[W.Bxn[r]])
        yield
        for _ in range(delay):
            yield
        for half in range(2):
            bk = banks[half]
            for k in range(4):
                kk = 4 * half + k
                S.op("pe", lambda e, bk=bk, k=k, kk=kk: e.transpose(out=psb(bk)[:, k, :n], in_=xn[:n, kk * 128:(kk + 1) * 128],
                                                                    identity=ident[:n, :n]),
                     reads=[W.Bxn[r], B_const], writes=[PB[bk]])
            eng = "act" if half == 0 else "dve"
            if eng == "act":
                S.op("act", lambda e, bk=bk, half=half: e.activation(out=dst_xnT[:, 4 * half:4 * half + 4, :n],
                                                                     in_=psb(bk)[:, :, :n], func=AF.Copy),
                     reads=[PB[bk]], writes=dst_bufs)
            else:
                S.op("dve", lambda e, bk=bk, half=half: e.tensor_copy(out=dst_xnT[:, 4 * half:4 * half + 4, :n],
                                                                      in_=psb(bk)[:, :, :n]),
                     reads=[PB[bk]], writes=dst_bufs)
            yield

    def qk_norm_rope(W, n, src_ps, nh, gmain, gswap, pos0, Bsrc):
        w = nh * 128
        r = W.cnt % 2
        W.cnt += 1
        ss, rs = W.ss[r], W.rs[r]
        tb = W.cnt % 2
        ld("sp", W.cos[tb][:n, :], cos_d[pos0:pos0 + n, :], [W.Btab[tb]])
        ld("sp", W.sin[tb][:n, :], sin_d[pos0:pos0 + n, :], [W.Btab[tb]])
        S.op("act", lambda e: e.activation(out=W.sq[:n, :w], in_=src_ps[:n, :w], func=AF.Square), reads=Bsrc, writes=[W.Bsq])
        yield
        S.op("dve", lambda e: e.tensor_reduce(out=ss[:n, 0:nh], in_=W.sq[:n, :w].rearrange("p (h d) -> p h d", h=nh),
                                              axis=AX.X, op=ALU.add), reads=[W.Bsq], writes=[W.Bss[r]])
        yield
        S.op("dve", lambda e: e.tensor_scalar(out=ss[:n, 0:nh], in0=ss[:n, 0:nh], scalar1=1.0 / HD, scalar2=EPS,
                                              op0=ALU.mult, op1=ALU.add), reads=[W.Bss[r]], writes=[W.Bss[r]])
        yield
        rsqrt_small(ss, rs, n, nh, W.Bss[r], W.Brs[r])
        yield
        S.op("dve", lambda e: e.tensor_tensor(out=W.qn[:n, :w].rearrange("p (h d) -> p h d", h=nh),
                                              in0=src_ps[:n, :w].rearrange("p (h d) -> p h d", h=nh),
                                              in1=rs[:n, 0:nh].unsqueeze(2).to_broadcast([n, nh, HD]), op=ALU.mult),
             reads=Bsrc + [W.Brs[r]], writes=[W.Bqn])
        yield
        S.op("dve", lambda e: e.tensor_tensor(out=W.t1[:n, :], in0=W.cos[tb][:n, :], in1=gmain[:n, :], op=ALU.mult),
             reads=[W.Btab[tb], B_const], writes=[W.Bt12])
        S.op("dve", lambda e: e.tensor_tensor(out=W.t2[:n, :], in0=W.sin[tb][:n, :], in1=gswap[:n, :], op=ALU.mult),
             reads=[W.Btab[tb], B_const], writes=[W.Bt12])
        yield
        S.op("dve", lambda e: e.tensor_tensor(out=W.m1[:n, :w].rearrange("p (h d) -> p h d", h=nh),
                                              in0=W.qn[:n, :w].rearrange("p (h d) -> p h d", h=nh),
                                              in1=W.t1[:n, :].unsqueeze(1).to_broadcast([n, nh, HD]), op=ALU.mult),
             reads=[W.Bqn, W.Bt12], writes=[W.Bm1])
        yield
        for hf in range(2):
            def v4(ap, width=w):
                return ap[:n, :width].rearrange("p (h a f d) -> p h a f d", h=nh, a=2, f=2)
            t2v = W.t2[:n, :].rearrange("p (a f d) -> p a f d", a=2, f=2)
            for a in range(2):
                S.op("dve", lambda e, hf=hf, a=a: e.tensor_tensor(
                    out=v4(W.m2)[:, :, a, hf, :], in0=v4(W.qn)[:, :, a, 1 - hf, :],
                    in1=t2v[:, a, hf, :].unsqueeze(1).to_broadcast([n, nh, 32]), op=ALU.mult),
                    reads=[W.Bqn, W.Bt12], writes=[W.Bm2])
            yield
        S.op("dve", lambda e: e.tensor_tensor(out=W.qr[:n, :w], in0=W.m1[:n, :w], in1=W.m2[:n, :w], op=ALU.add),
             reads=[W.Bm1, W.Bm2], writes=[W.Bqr])
        yield

    def drain(g):
        for _ in g:
            pass

    def dump(name, ap2d, bufs):
        ld("pool", dbg[name], ap2d, [Buf("dump")], rbufs=bufs)

    def rev(ap):
        return bass.AP(ap.tensor, ap.offset + T - 1, [list(ap.ap[0]), [-1, T]])

    for b in range(nb):
        A.ptr = WORK0
        W = alloc_tilework(nx=3, with_q=False)
        S.op("pool", lambda e: e.memset(xnT[:, :, 0:2], 0.0), writes=[B_xpad])
        S.op("pool", lambda e: e.memset(xnT[:, :, T + 2:T + 4], 0.0), writes=[B_xpad])
        load_x(W, 0, b, 0)
        load_x(W, 1, b, 1)
        for i in range(17):
            if i + 2 < 17:
                load_x(W, (i + 2) % 3, b, i + 2)
            s, n = tile_cols(i)
            drain(norm1_tile(W, i % 3, n, xnT[:, :, 2 + s:2 + s + n], [B_xnT[i]], (6, 7),
                             stats=(rstd1_all[:, i:i + 1], B_rstd1[i])))
        if debug and b == 0:
            dump("d_xnT", xnT.rearrange("p a b -> p (a b)"), B_xnT + [B_xpad])
        S.barrier()

        A.ptr = WORK0
        xc = A.f32([128, T], "xc")
        xcb = A.bf16([128, T], "xcb")
        av = A.f32([128, T], "av")
        a2 = A.f32([128, T], "a2")
        hf = [A.f32([128, T], "hf%d" % i) for i in range(2)]
        hb = A.f32([128, T], "hb")
        B_xc, B_xcb, B_hb, B_a, B_a2 = Buf("xc"), Buf("xcb"), Buf("hb"), Buf("a"), Buf("a2")
        B_hf = [Buf("hf0"), Buf("hf1")]
        wst = [A.bf16([128, 8, 384], "wst%d" % i) for i in range(2)]
        B_wst = [Buf("wst0"), Buf("wst1")]
        tmp = [A.f32([128, 512], "tmp%d" % i) for i in range(6)]
        B_tmp = [Buf("tmp%d" % i) for i in range(6)]

        B_wgg = [Buf("wgg0"), Buf("wgg1")]

        def load_wxr(c):
            sl = c % 2
            for k in range(8):
                ld("pool", wst[sl][:, k, 0:128], win_v[:, k, C_XR + c * 128:C_XR + (c + 1) * 128], [B_wst[sl]])

        def load_wgg(c):
            sl = c % 2
            for k in range(8):
                ld("pool", wst[sl][:, k, 128:256], win_v[:, k, C_GR + c * 128:C_GR + (c + 1) * 128], [B_wgg[sl]])
                ld("pool", wst[sl][:, k, 256:384], win_v[:, k, C_GN + c * 128:C_GN + (c + 1) * 128], [B_wgg[sl]])

        def rnn_proj_conv(c):
            sl = c % 2
            for w in range(NWIN):
                p0 = WIN * w
                nout = min(WIN, T - p0)
                nin = nout + 3
                bk = w % 2
                for k in range(8):
                    S.op("pe", lambda e: e.matmul(psf(bk)[:, :nin], lhsT=wst[sl][:, k, 0:128], rhs=xnT[:, k, p0:p0 + nin],
                                                  start=(k == 0), stop=(k == 7)),
                         reads=[B_wst[sl], B_xpad] + [B_xnT[i] for i in tiles_overlapping(p0 - 2, p0 + nout + 1)],
                         writes=[PB[bk]])
                S.op("dve", lambda e: e.tensor_scalar(out=xc[:, p0:p0 + nout], in0=psf(bk)[:, 0:nout], scalar1=cw[:, 0, c:c + 1],
                                                      scalar2=cb[:, c:c + 1], op0=ALU.mult, op1=ALU.add),
                     reads=[PB[bk], B_const], writes=[B_xc])
                yield
                for j in range(1, 4):
                    S.op("dve", lambda e: e.scalar_tensor_tensor(out=xc[:, p0:p0 + nout], in0=psf(bk)[:, j:j + nout], scalar=cw[:, j, c:c + 1],
                                                                 in1=xc[:, p0:p0 + nout], op0=ALU.mult, op1=ALU.add),
                         reads=[PB[bk], B_const, B_xc], writes=[B_xc])
                    yield
                S.op("act", lambda e: e.activation(out=xcb[:, p0:p0 + nout], in_=xc[:, p0:p0 + nout], func=AF.Copy),
                     reads=[B_xc], writes=[B_xcb])
            yield

        def rnn_dir(c, d):
            ci = d * 8 + c

            def exps(w):
                p0 = WIN * w
                nout = min(WIN, T - p0)
                tr = w % 2
                S.op("act", lambda e: e.activation(out=av[:, p0:p0 + nout], in_=tmp[tr][:, :nout], func=AF.Exp,
                                                   scale=cch[:, ci:ci + 1], bias=cch[:, ci:ci + 1]),
                     reads=[B_tmp[tr], B_const], writes=[B_a])
                S.op("act", lambda e: e.activation(out=a2[:, p0:p0 + nout], in_=tmp[tr][:, :nout], func=AF.Exp,
                                                   scale=cc[:, ci:ci + 1], bias=cc[:, ci:ci + 1]),
                     reads=[B_tmp[tr], B_const], writes=[B_a2])
            for w in range(NWIN):
                p0 = WIN * w
                nout = min(WIN, T - p0)
                bk = 2 + (w % 2)
                tr = w % 2
                S.op("pe", lambda e: e.matmul(psf(bk)[:, :nout], lhsT=wab[:, ci, :], rhs=xcb[:, p0:p0 + nout], start=True, stop=True),
                     reads=[B_xcb, B_const], writes=[PB[bk]])
                S.op("act", lambda e: e.activation(out=tmp[tr][:, :nout], in_=psf(bk)[:, :nout], func=AF.Tanh, scale=0.5,
                                                   bias=bah[:, ci:ci + 1]), reads=[PB[bk], B_const], writes=[B_tmp[tr]])
                if w > 0:
                    exps(w - 1)
                yield
            exps(NWIN - 1)
            yield
            S.op("act", lambda e: e.activation(out=a2, in_=a2, func=AF.Sqrt, scale=-1.0, bias=1.0), reads=[B_a2], writes=[B_a2])
            S.op("dve", lambda e: e.scalar_tensor_tensor(out=a2, in0=a2, scalar=0.125, in1=xc, op0=ALU.mult, op1=ALU.mult),
                 reads=[B_a2, B_xc], writes=[B_a2])
            yield
            for w in range(NWIN):
                p0 = WIN * w
                nout = min(WIN, T - p0)
                bk = 2 + (w % 2)
                tr = w % 2
                S.op("pe", lambda e: e.matmul(psf(bk)[:, :nout], lhsT=wxb[:, ci, :], rhs=xcb[:, p0:p0 + nout], start=True, stop=True),
                     reads=[B_xcb, B_const], writes=[PB[bk]])
                S.op("act", lambda e: e.activation(out=tmp[tr][:, :nout], in_=psf(bk)[:, :nout], func=AF.Tanh, scale=0.5,
                                                   bias=bxh[:, ci:ci + 1]), reads=[PB[bk], B_const], writes=[B_tmp[tr]])
                S.op("dve", lambda e: e.scalar_tensor_tensor(out=a2[:, p0:p0 + nout], in0=tmp[tr][:, :nout], scalar=1.0,
                                                             in1=a2[:, p0:p0 + nout], op0=ALU.add, op1=ALU.mult),
                     reads=[B_a2, B_tmp[tr]], writes=[B_a2])
                yield
            if d == 0:
                hfc = hf[c % 2]
                S.op("dve", lambda e: e.tensor_tensor_scan(out=hfc, data0=av, data1=a2, initial=0.0, op0=ALU.mult, op1=ALU.add),
                     reads=[B_a, B_a2], writes=[B_hf[c % 2]])
            else:
                S.op("dve", lambda e: e.tensor_tensor_scan(out=rev(hb), data0=rev(av), data1=rev(a2), initial=0.0,
                                                           op0=ALU.mult, op1=ALU.add),
                     reads=[B_a, B_a2], writes=[B_hb])
            yield

        def rnn_combine(c):
            sl = c % 2
            hfc, Bhf = hf[c % 2], B_hf[c % 2]
            for w in range(NWIN):
                p0 = max(WIN * w, N_META)
                p1 = min(WIN * (w + 1), T)
                nout = p1 - p0
                par = w % 2
                b4, b5 = 4 + 2 * par, 5 + 2 * par
                g4, g5, Bg4, Bg5 = tmp[2 + 2 * par], tmp[3 + 2 * par], B_tmp[2 + 2 * par], B_tmp[3 + 2 * par]
                tl = list(range((p0 - N_META) // 128, (p1 - 1 - N_META) // 128 + 1))
                xb = [B_xnT[i] for i in tiles_overlapping(p0, p1)]
                for k in range(8):
                    S.op("pe", lambda e: e.matmul(psf(b4)[:, :nout], lhsT=wst[sl][:, k, 128:256], rhs=xnT[:, k, 2 + p0:2 + p0 + nout],
                                                  start=(k == 0), stop=(k == 7)), reads=[B_wgg[sl]] + xb, writes=[PB[b4]])
                for k in range(8):
                    S.op("pe", lambda e: e.matmul(psf(b5)[:, :nout], lhsT=wst[sl][:, k, 256:384], rhs=xnT[:, k, 2 + p0:2 + p0 + nout],
                                                  start=(k == 0), stop=(k == 7)), reads=[B_wgg[sl]] + xb, writes=[PB[b5]])
                S.op("act", lambda e: e.activation(out=g4[:, :nout], in_=psf(b4)[:, :nout], func=AF.Square), reads=[PB[b4]], writes=[Bg4])
                S.op("act", lambda e: e.activation(out=g5[:, :nout], in_=psf(b5)[:, :nout], func=AF.Tanh, scale=0.5),
                     reads=[PB[b5]], writes=[Bg5])
                S.op("dve", lambda e: e.tensor_scalar(out=g4[:, :nout], in0=g4[:, :nout], scalar1=0.044715 * GK, scalar2=GK,
                                                      op0=ALU.mult, op1=ALU.add), reads=[Bg4], writes=[Bg4])
                yield
                S.op("dve", lambda e: e.tensor_tensor(out=g4[:, :nout], in0=g4[:, :nout], in1=psf(b4)[:, :nout], op=ALU.mult),
                     reads=[Bg4, PB[b4]], writes=[Bg4])
                yield
                S.op("act", lambda e: e.activation(out=g4[:, :nout], in_=g4[:, :nout], func=AF.Tanh), reads=[Bg4], writes=[Bg4])
                S.op("dve", lambda e: e.tensor_tensor(out=hb[:, p0:p0 + nout], in0=hfc[:, p0:p0 + nout], in1=hb[:, p0:p0 + nout], op=ALU.add),
                     reads=[Bhf, B_hb], writes=[B_hb])
                yield
                S.op("dve", lambda e: e.scalar_tensor_tensor(out=g4[:, :nout], in0=g4[:, :nout], scalar=1.0, in1=psf(b4)[:, :nout],
                                                             op0=ALU.add, op1=ALU.mult), reads=[Bg4, PB[b4]], writes=[Bg4])
                yield
                S.op("dve", lambda e: e.tensor_tensor(out=g4[:, :nout], in0=g4[:, :nout], in1=hb[:, p0:p0 + nout], op=ALU.mult),
                     reads=[Bg4, B_hb], writes=[Bg4])
                yield
                S.op("dve", lambda e: e.scalar_tensor_tensor(out=mixr[:, c, p0 - N_META:p0 - N_META + nout], in0=g5[:, :nout], scalar=1.0,
                                                             in1=g4[:, :nout], op0=ALU.add, op1=ALU.mult),
                     reads=[Bg4, Bg5], writes=[B_mixr[c][i] for i in tl])
                yield

        def merge(ga, gb, ratio):
            for _ in ga:
                for _ in range(ratio):
                    next(gb, None)
            drain(gb)

        load_wxr(0)
        load_wgg(0)
        drain(rnn_proj_conv(0))
        for c in range(8):
            if c + 1 < 8:
                load_wxr(c + 1)
            merge(rnn_dir(c, 0), rnn_combine(c - 1) if c > 0 else iter(()), 3)
            if c + 1 < 8:
                load_wgg(c + 1)
            drain(rnn_dir(c, 1))
            if c + 1 < 8:
                drain(rnn_proj_conv(c + 1))
        drain(rnn_combine(7))
        if debug and b == 0:
            dump("d_mixr", mixr.rearrange("p a b -> p (a b)"), [x for row in B_mixr for x in row])
        S.barrier()

        A.ptr = WORK0
        KT = A.bf16([128, 2, T], "KT")
        V = A.bf16([128, 17, 256], "V")
        wkv = A.bf16([128, 8, 512], "wkv")
        B_wkv = Buf("wkv")
        for k in range(8):
            ld("pool", wkv[:, k, :], win_v[:, k, C_K:C_K + 512], [B_wkv])
        P3_BASE = A.ptr
        W = alloc_tilework(nx=3, with_q=True)
        sqf = A.f32([128, D], "sqf")
        B_sqf = Buf("sqf")
        for i in range(17):
            s, n = tile_cols(i)
            for k in range(8):
                S.op("pe", lambda e, k=k, s=s, n=n: e.matmul(psf(4)[:n, :], lhsT=xnT[:, k, 2 + s:2 + s + n], rhs=wkv[:, k, :],
                                                             start=(k == 0), stop=(k == 7)),
                     reads=[B_xnT[i], B_wkv], writes=[PB[4]])
            S.op("act", lambda e, i=i, n=n: e.activation(out=V[:n, i, :], in_=psf(4)[:n, 256:512], func=AF.Copy),
                 reads=[PB[4]], writes=[B_V[i]])
            drain(qk_norm_rope(W, n, psf(4), 2, gk, gks, s, [PB[4]]))
            for h in range(2):
                S.op("pe", lambda e, h=h, n=n: e.transpose(out=psb(5)[:, h, :n], in_=W.qr[:n, h * 128:(h + 1) * 128],
                                                           identity=ident[:n, :n]),
                     reads=[W.Bqr, B_const], writes=[PB[5]])
            S.op("act", lambda e, s=s, n=n: e.activation(out=KT[:, :, s:s + n], in_=psb(5)[:, 0:2, :n], func=AF.Copy),
                 reads=[PB[5]], writes=[B_KT[i]])
        if debug and b == 0:
            dump("d_KT", KT.rearrange("p a b -> p (a b)"), B_KT)
            dump("d_V", V.rearrange("p a b -> p (a b)"), B_V)

        ht = [A.f32([128, D], "ht%d" % i) for i in range(2)]
        B_ht = [Buf("ht0"), Buf("ht1")]
        S.barrier()
        save_ptr = A.ptr
        A.ptr = XNT0
        xnTb = [A.bf16([128, 8, 128], "xnTb%d" % i) for i in range(2)]
        B_xnTb = [Buf("xnTb0"), Buf("xnTb1")]
        QTb = [A.bf16([128, 8, 128], "QTb%d" % i) for i in range(2)]
        B_QTb = [Buf("QTb0"), Buf("QTb1")]
        eg = [A.f32([128, 8, 128], "eg%d" % i) for i in range(2)]
        B_eg = [Buf("eg0"), Buf("eg1")]
        PT = [A.bf16([128, 512], "PT%d" % i) for i in range(4)]
        B_PT = [Buf("PT%d" % i) for i in range(4)]
        rr = [A.f32([128, 512], "rr%d" % i) for i in range(2)]
        B_rr = [Buf("rr0"), Buf("rr1")]
        mixb = [A.bf16([128, 8, 128], "mixb%d" % i) for i in range(2)]
        B_mixb = [Buf("mixb0"), Buf("mixb1")]
        assert A.ptr <= XNT1, (A.ptr, XNT1)
        A.ptr = save_ptr
        pt_cnt = [0]

        def prep(qb):
            sl = qb % 2
            pos0 = N_META + qb * 128
            yield from norm1_tile(W, qb % 3, 128, xnTb[sl], [B_xnTb[sl]], (6, 7),
                                  reuse=(rstd1_all[:, qb + 1:qb + 2], B_rstd1[qb + 1]), delay=2)
            yield
            yield
            for half in range(2):
                for k in range(8):
                    S.op("pe", lambda e: e.matmul(psf(4 + half), lhsT=xnTb[sl][:, k, :], rhs=wq[:, k, half * 512:(half + 1) * 512],
                                                  start=(k == 0), stop=(k == 7)),
                         reads=[B_xnTb[sl], B_wq], writes=[PB[4 + half]])
                yield
            yield from qk_norm_rope(W, 128, psum[:, 4:6, :].rearrange("p a b -> p (a b)"), 8, gq, gqs, pos0, [PB[4], PB[5]])
            yield
            yield
            yield
            for half in range(2):
                bk = 6 + half
                for k in range(4):
                    hh = 4 * half + k
                    S.op("pe", lambda e: e.transpose(out=psb(bk)[:, k, :], in_=W.qr[:, hh * 128:(hh + 1) * 128], identity=ident),
                         reads=[W.Bqr, B_const], writes=[PB[bk]])
                if half == 0:
                    S.op("act", lambda e: e.activation(out=QTb[sl][:, 0:4, :], in_=psb(bk), func=AF.Copy),
                         reads=[PB[bk]], writes=[B_QTb[sl]])
                else:
                    S.op("dve", lambda e: e.tensor_copy(out=QTb[sl][:, 4:8, :], in_=psb(bk)), reads=[PB[bk]], writes=[B_QTb[sl]])
                yield
            yield
            yield
            for half in range(2):
                bk = 6 + half
                for hh in range(4):
                    h = 4 * half + hh
                    for k in range(8):
                        S.op("pe", lambda e: e.matmul(psf(bk)[:, hh * 128:(hh + 1) * 128], lhsT=wga[:, k, h * 128:(h + 1) * 128],
                                                      rhs=xnTb[sl][:, k, :], start=(k == 0), stop=(k == 7)),
                             reads=[B_xnTb[sl], B_wga], writes=[PB[bk]])
                    if hh % 2 == 1:
                        yield
                S.op("act", lambda e: e.activation(out=eg[sl][:, 4 * half:4 * half + 4, :].rearrange("p a b -> p (a b)"),
                                                   in_=psf(bk), func=AF.Exp, scale=-1.0),
                     reads=[PB[bk]], writes=[B_eg[sl]])
                yield

        def post(qb):
            sl = qb % 2
            gi = b * NT_TILES + qb
            for _ in range(6):
                yield
            for half in range(2):
                for c in range(8):
                    S.op("pe", lambda e: e.matmul(psf(4 + half), lhsT=mixb[sl][:, c, :], rhs=wo[:, c, half * 512:(half + 1) * 512],
                                                  start=(c == 0), stop=(c == 7)),
                         reads=[B_mixb[sl], B_wo], writes=[PB[4 + half]])
                yield
            S.op("dve", lambda e: e.tensor_tensor(out=ht[sl], in0=psum[:, 4:6, :].rearrange("p a b -> p (a b)"), in1=W.xt[qb % 3], op=ALU.add),
                 reads=[PB[4], PB[5], W.Bxt[qb % 3]], writes=[B_ht[sl]])
            yield
            ld("sp", hbuf_d[gi * 128:(gi + 1) * 128, :], ht[sl], [B_hbuf[gi]], rbufs=[B_ht[sl]])
            r2 = W.cnt % 2
            W.cnt += 1
            S.op("pool", lambda e: e.tensor_tensor(out=sqf, in0=ht[sl], in1=ht[sl], op=ALU.mult), reads=[B_ht[sl]], writes=[B_sqf])
            yield
            S.op("dve", lambda e: e.tensor_reduce(out=W.ss[r2][:, 0:1], in_=sqf, axis=AX.X, op=ALU.add), reads=[B_sqf], writes=[W.Bss[r2]])
            yield
            S.op("dve", lambda e: e.tensor_scalar(out=W.ss[r2][:, 0:1], in0=W.ss[r2][:, 0:1], scalar1=1.0 / D, scalar2=EPS,
                                                  op0=ALU.mult, op1=ALU.add), reads=[W.Bss[r2]], writes=[W.Bss[r2]])
            yield
            S.op("act", lambda e: e.activation(out=W.ss[r2][:, 0:1], in_=W.ss[r2][:, 0:1], func=AF.Ln), reads=[W.Bss[r2]], writes=[W.Bss[r2]])
            S.op("act", lambda e: e.activation(out=rstd2_all[:, gi:gi + 1], in_=W.ss[r2][:, 0:1], func=AF.Exp, scale=-0.5),
                 reads=[W.Bss[r2]], writes=[B_rstd2[gi]])
            yield

        def chain(*gens):
            for g in gens:
                if g is not None:
                    yield from g

        load_x(W, 0, b, 1)
        drain(prep(0))
        load_x(W, 1, b, 2)
        for qb in range(NT_TILES):
            sl = qb % 2
            nxt = chain(post(qb - 1) if qb > 0 else None, prep(qb + 1) if qb + 1 < NT_TILES else None)
            for j in range(NKV):
                qg_ap = QTb[sl][:, 4 * j:4 * j + 4, :].rearrange("p a b -> p (a b)")

                def s_mm(kc, j=j, qg_ap=qg_ap):
                    ks, nk = tile_cols(kc)
                    sb_ = kc % 2
                    S.op("pe", lambda e: e.matmul(psf(sb_)[:nk, :], lhsT=KT[:, j, ks:ks + nk], rhs=qg_ap, start=True, stop=True),
                         reads=[B_KT[kc], B_QTb[sl]], writes=[PB[sb_]])

                s_mm(0)
                for kc in range(17):
                    ks, nk = tile_cols(kc)
                    sb_ = kc % 2
                    pi = pt_cnt[0] % 4
                    pt_cnt[0] += 1
                    if kc + 1 < 17:
                        s_mm(kc + 1)
                    next(nxt, None)
                    S.op("act", lambda e: e.activation(out=PT[pi][:nk, :], in_=psf(sb_)[:nk, :], func=AF.Exp, scale=SM_SCALE),
                         reads=[PB[sb_]], writes=[B_PT[pi]])
                    S.op("pe", lambda e: e.matmul(psf(2), lhsT=V[:nk, kc, j * 128:(j + 1) * 128], rhs=PT[pi][:nk, :],
                                                  start=(kc == 0), stop=(kc == 16)),
                         reads=[B_V[kc], B_PT[pi]], writes=[PB[2]])
                    S.op("pe", lambda e: e.matmul(psf(3), lhsT=ones[:nk, :], rhs=PT[pi][:nk, :],
                                                  start=(kc == 0), stop=(kc == 16)),
                         reads=[B_const, B_PT[pi]], writes=[PB[3]])
                    next(nxt, None)
                r = j % 2
                eg_ap = eg[sl][:, 4 * j:4 * j + 4, :].rearrange("p a b -> p (a b)")
                S.op("dve", lambda e, r=r, eg_ap=eg_ap: e.scalar_tensor_tensor(out=rr[r], in0=eg_ap, scalar=1.0, in1=psf(3),
                                                                               op0=ALU.add, op1=ALU.mult),
                     reads=[B_eg[sl], PB[3]], writes=[B_rr[r]])
                S.op("dve", lambda e, r=r: e.reciprocal(out=rr[r], in_=rr[r]), reads=[B_rr[r]], writes=[B_rr[r]])
                S.op("dve", lambda e, r=r: e.tensor_tensor(out=rr[r], in0=psf(2), in1=rr[r], op=ALU.mult),
                     reads=[B_rr[r], PB[2]], writes=[B_rr[r]])
                S.op("dve", lambda e, r=r, j=j: e.tensor_tensor(
                    out=mixb[sl][:, 4 * j:4 * j + 4, :], in0=rr[r].rearrange("p (a b) -> p a b", a=4),
                    in1=mixr[:, 4 * j:4 * j + 4, qb * 128:(qb + 1) * 128], op=ALU.add),
                    reads=[B_rr[r]] + [B_mixr[c][qb] for c in range(4 * j, 4 * j + 4)], writes=[B_mixb[sl]])
            drain(nxt)
            if qb + 2 < NT_TILES:
                load_x(W, (qb + 2) % 3, b, qb + 3)
        drain(post(NT_TILES - 1))
        S.barrier()

    A.release()
    A.mark()
    S.barrier()
    g2b = A.f32([128, D], "g2b")
    ld("sp", g2b, bc(n2_d[0:1, :], D), [B_const])
    w1 = A.bf16([128, 8, 2 * D_FF], "w1")
    w2 = A.bf16([128, NFF, D], "w2")
    B_w1 = [Buf("w1_%d" % j) for j in range(NFF)]
    B_w2 = [Buf("w2_%d" % j) for j in range(NFF)]
    for k in range(8):
        ld("pool", w1[:, k, :], w1_v[:, k, :], B_w1)
    for j in range(NFF):
        ld("pool", w2[:, j, :], w2_v[:, j, :], [B_w2[j]])
    GT = 2
    hB = [A.f32([128, GT, D], "hB%d" % i) for i in range(2)]
    B_hB = [[Buf("hB%d_%d" % (i, t)) for t in range(GT)] for i in range(2)]
    hn = [A.bf16([128, D], "hn%d" % i) for i in range(2)]
    B_hn = [Buf("hn0"), Buf("hn1")]
    hnT = [A.bf16([128, 8, GT * 128], "hnT%d" % i) for i in range(2)]
    B_hnT = [Buf("hnT0"), Buf("hnT1")]
    actT = A.bf16([128, NFF, GT * 128], "actT")
    B_act = [Buf("act%d" % j) for j in range(NFF)]
    sg = [A.f32([128, GT * 128], "sg%d" % i) for i in range(3)]
    B_sg = [Buf("sg%d" % i) for i in range(3)]
    n_groups = nb * NT_TILES // GT
    out_flat = out_d.rearrange("b s d -> (b s) d")
    out_ops = []

    def load_h(g):
        for t in range(GT):
            gi = g * GT + t
            ld("sp", hB[g % 2][:, t, :], hbuf_d[gi * 128:(gi + 1) * 128, :], [B_hB[g % 2][t]], rbufs=[B_hbuf[gi]])

    load_h(0)
    for g in range(n_groups):
        sl = g % 2
        if g + 1 < n_groups:
            load_h(g + 1)
        for t in range(GT):
            gi = g * GT + t
            r = t % 2
            S.op("dve", lambda e, t=t, gi=gi, r=r: e.scalar_tensor_tensor(out=hn[r], in0=hB[sl][:, t, :], scalar=rstd2_all[:, gi:gi + 1],
                                                                          in1=g2b, op0=ALU.mult, op1=ALU.mult),
                 reads=[B_hB[sl][t], B_rstd2[gi], B_const], writes=[B_hn[r]])
            for half in range(2):
                bk = 6 + half
                for k in range(4):
                    kk = 4 * half + k
                    S.op("pe", lambda e, bk=bk, k=k, kk=kk, r=r: e.transpose(out=psb(bk)[:, k, :], in_=hn[r][:, kk * 128:(kk + 1) * 128],
                                                                             identity=ident), reads=[B_hn[r], B_const], writes=[PB[bk]])
                if half == 0:
                    S.op("act", lambda e, bk=bk, t=t: e.activation(out=hnT[sl][:, 0:4, t * 128:(t + 1) * 128], in_=psb(bk), func=AF.Copy),
                         reads=[PB[bk]], writes=[B_hnT[sl]])
                else:
                    S.op("dve", lambda e, bk=bk, t=t: e.tensor_copy(out=hnT[sl][:, 4:8, t * 128:(t + 1) * 128], in_=psb(bk)),
                         reads=[PB[bk]], writes=[B_hnT[sl]])
        NTOK = GT * 128
        for j in range(NFF):
            bk = j % 4
            for k in range(8):
                S.op("pe", lambda e, bk=bk, j=j, k=k: e.matmul(psf(bk)[:, 0:NTOK], lhsT=w1[:, k, j * 128:(j + 1) * 128], rhs=hnT[sl][:, k, :],
                                                               start=(k == 0), stop=(k == 7)), reads=[B_w1[j], B_hnT[sl]], writes=[PB[bk]])
            for k in range(8):
                S.op("pe", lambda e, bk=bk, j=j, k=k: e.matmul(psf(bk)[:, NTOK:2 * NTOK], lhsT=w1[:, k, D_FF + j * 128:D_FF + (j + 1) * 128],
                                                               rhs=hnT[sl][:, k, :], start=(k == 0), stop=(k == 7)),
                     reads=[B_w1[j], B_hnT[sl]], writes=[PB[bk]])
            si = j % 3
            S.op("act", lambda e, bk=bk, si=si: e.activation(out=sg[si], in_=psf(bk)[:, 0:NTOK], func=AF.Silu),
                 reads=[PB[bk]], writes=[B_sg[si]])
            S.op("dve", lambda e, bk=bk, si=si, j=j: e.tensor_tensor(out=actT[:, j, :], in0=sg[si], in1=psf(bk)[:, NTOK:2 * NTOK], op=ALU.mult),
                 reads=[B_sg[si], PB[bk]], writes=[B_act[j]])
        for t in range(GT):
            gi = g * GT + t
            for half in range(2):
                bk = 4 + half
                for j in range(NFF):
                    S.op("pe", lambda e, bk=bk, j=j, t=t, half=half: e.matmul(psf(bk), lhsT=actT[:, j, t * 128:(t + 1) * 128],
                                                                               rhs=w2[:, j, half * 512:(half + 1) * 512],
                                                                               start=(j == 0), stop=(j == NFF - 1)),
                         reads=[B_act[j], B_w2[j]], writes=[PB[bk]])
            o = S.op("dve", lambda e, t=t: e.tensor_tensor(out=hB[sl][:, t, :], in0=psum[:, 4:6, :].rearrange("p a b -> p (a b)"),
                                                           in1=hB[sl][:, t, :], op=ALU.add),
                     reads=[PB[4], PB[5], B_hB[sl][t]], writes=[B_hB[sl][t]])
            st = ld("sp", out_flat[gi * 128:(gi + 1) * 128, :], hB[sl][:, t, :], [Buf("ost")], rbufs=[B_hB[sl][t]])
            out_ops.append(st)
    S.op("sp", None, extra=out_ops)
    S.barrier()

    sem_cms = [nc.semaphore("s_" + e) for e in ENGS] + [nc.semaphore("d%d" % i) for i in range(N_DMA_SEMS)]
    sem_objs = [c.__enter__() for c in sem_cms]
    sems = dict(zip(ENGS, sem_objs[:len(ENGS)]))
    dsems = sem_objs[len(ENGS):]
    with nc.Block() as block:
        S.emit(nc, block, sems, dsems)
    for c in reversed(sem_cms):
        c.__exit__(None, None, None)
    psum_cm.__exit__(None, None, None)
    arena_cm.__exit__(None, None, None)
    return nc


_CONST_CACHE = {}


def make_in_maps(inputs, n_cores=N_CORES):
    if "rope" not in _CONST_CACHE:
        _CONST_CACHE["rope"] = rope_tables()
        _CONST_CACHE["ident"] = np.eye(128, dtype=np.float32)
    cos, sin = _CONST_CACHE["rope"]
    x = np.ascontiguousarray(inputs["x"], dtype=np.float32)
    maps = []
    for c in range(n_cores):
        m = {k: np.ascontiguousarray(v, dtype=np.float32) for k, v in inputs.items() if k != "x"}
        m["x"] = np.ascontiguousarray(x[c * NB_CORE:(c + 1) * NB_CORE])
        m["rope_cos"] = cos
        m["rope_sin"] = sin
        m["ident"] = _CONST_CACHE["ident"]
        maps.append(m)
    return maps


def kernel(**inputs):
    nc = build()
    maps = make_in_maps(inputs)
    res = run_bass_kernel_spmd(nc, maps, core_ids=list(range(N_CORES)))
    outs = [np.asarray(r["out"], dtype=np.float32) for r in res.results]
    return np.concatenate(outs, axis=0)
```

```python
import types
import numpy as np
import concourse.bass as bass
import concourse.mybir as mybir
from concourse.bass_utils import run_bass_kernel_spmd

F32 = mybir.dt.float32
BF16 = mybir.dt.bfloat16
ALU = mybir.AluOpType
AF = mybir.ActivationFunctionType
AX = mybir.AxisListType

N_CORES = 8
BATCH, SEQ, D = 32, 2048, 1024
NB_CORE = BATCH // N_CORES
N_META = 16
T = SEQ + N_META
NT_TILES = SEQ // 128
HD = 128
NQ, NKV = 8, 2
D_FF = 2816
NFF = D_FF // 128
IN_W = 5632
EPS = 1e-6
C_Q, C_K, C_V, C_XR, C_GR, C_GA, C_GN = 0, 1024, 1280, 1536, 2560, 3584, 4608
WIN = 509
NWIN = 5
GK = 0.7978845608028654
SM_SCALE = HD ** -0.5


class Buf:
    __slots__ = ("name", "w", "r")

    def __init__(self, name):
        self.name = name
        self.w = None
        self.r = []


class Op:
    __slots__ = ("eng", "fn", "deps", "sig", "sem", "val", "dma")


def _freeze(fn):
    if fn is None or fn.__closure__ is None:
        return fn
    cells = []
    for c in fn.__closure__:
        try:
            cells.append(types.CellType(c.cell_contents))
        except ValueError:
            cells.append(c)
    return types.FunctionType(fn.__code__, fn.__globals__, fn.__name__, fn.__defaults__, tuple(cells))


ENGS = ("pe", "act", "dve", "pool", "sp")
N_DMA_SEMS = 40


class Sched:
    def __init__(self):
        self.ops = {e: [] for e in ENGS}
        self.dma_ops = []

    def op(self, eng, fn, reads=(), writes=(), dma=False, extra=()):
        o = Op()
        o.eng, o.fn, o.dma, o.sig, o.sem, o.val = eng, _freeze(fn), dma, False, None, 0
        deps = list(extra)
        for b in reads:
            if b.w is not None:
                deps.append(b.w)
        for b in writes:
            if b.w is not None:
                deps.append(b.w)
            deps.extend(b.r)
        seen, out = set(), []
        for d in deps:
            if id(d) in seen or d is o:
                continue
            seen.add(id(d))
            if d.eng == "pe" and eng == "pe" and not d.dma:
                continue
            out.append(d)
            d.sig = True
        o.deps = out
        for b in reads:
            b.r.append(o)
        for b in writes:
            b.w = o
            b.r = []
        self.ops[eng].append(o)
        if dma:
            self.dma_ops.append(o)
        return o

    def barrier(self):
        last = []
        for e in ENGS:
            for o in reversed(self.ops[e]):
                if o.fn is not None and not o.dma:
                    last.append(o)
                    break
        dmas = [o for o in self.dma_ops if o.eng == "sp"][-N_DMA_SEMS // 2:] + \
               [o for o in self.dma_ops if o.eng == "pool"][-N_DMA_SEMS // 2:]
        for e in ENGS:
            self.op(e, None, extra=last + dmas)

    def emit(self, nc, block, sems, dsems):
        for e in ENGS:
            cnt = 0
            for o in self.ops[e]:
                if o.dma or not o.sig:
                    continue
                cnt += 1
                o.sem, o.val = sems[e], cnt
        half = len(dsems) // 2
        pools = {"sp": dsems[:half], "pool": dsems[half:]}
        use = {id(x): 0 for x in dsems}
        prev = {id(x): None for x in dsems}
        cnt = {"sp": 0, "pool": 0}
        for o in self.dma_ops:
            pl = pools[o.eng]
            sm = pl[cnt[o.eng] % len(pl)]
            cnt[o.eng] += 1
            use[id(sm)] += 1
            o.sem, o.val = sm, 16 * use[id(sm)]
            if prev[id(sm)] is not None:
                o.deps.append(prev[id(sm)])
            prev[id(sm)] = o

        def run(eng_name, eng):
            waited = {}
            for o in self.ops[eng_name]:
                for d in o.deps:
                    key = id(d.sem)
                    if waited.get(key, 0) < d.val:
                        eng.wait_ge(d.sem, d.val)
                        waited[key] = d.val
                if o.fn is None:
                    continue
                inst = o.fn(eng)
                if o.dma:
                    inst.then_inc(o.sem, 16)
                elif o.sig:
                    inst.then_inc(o.sem, 1)

        @block.tensor
        def _(e):
            run("pe", e)

        @block.scalar
        def _(e):
            run("act", e)

        @block.vector
        def _(e):
            run("dve", e)

        @block.gpsimd
        def _(e):
            run("pool", e)

        @block.sync
        def _(e):
            run("sp", e)


def rope_tables():
    rows = SEQ // 64
    row = np.repeat(np.arange(rows, dtype=np.float32), 64)
    col = np.tile(np.arange(64, dtype=np.float32), rows)
    z = np.zeros((N_META,), np.float32)
    row = np.concatenate([z, row])
    col = np.concatenate([z, col])
    inv = np.exp(-np.log(np.float32(10000.0)) * np.arange(32, dtype=np.float32) / np.float32(32)).astype(np.float32)
    ar = (row[:, None] * inv[None, :]).astype(np.float32)
    ac = (col[:, None] * inv[None, :]).astype(np.float32)
    cos = np.concatenate([np.cos(ar), np.cos(ar), np.cos(ac), np.cos(ac)], axis=1).astype(np.float32)
    sin = np.concatenate([-np.sin(ar), np.sin(ar), -np.sin(ac), np.sin(ac)], axis=1).astype(np.float32)
    return np.ascontiguousarray(cos), np.ascontiguousarray(sin)


def build(nb=NB_CORE, debug=False):
    nc = bass.Bass("TRN2", target_bir_lowering=False)
    S = Sched()

    def din(name, shape):
        return nc.dram_tensor(name, list(shape), F32, kind="ExternalInput").ap()

    x_d = din("x", [NB_CORE, SEQ, D])
    meta_d = din("meta_tokens", [N_META, D])
    n1_d = din("norm1_g", [1, D])
    win_d = din("w_in", [1, D, IN_W])
    cw_d = din("conv_w", [1, 4, D])
    cb_d = din("conv_b", [1, D])
    wa_d = din("rg_wa", [1, 2, 8, 128, 128])
    ba_d = din("rg_ba", [1, 2, D])
    wx_d = din("rg_wx", [1, 2, 8, 128, 128])
    bx_d = din("rg_bx", [1, 2, D])
    lam_d = din("rg_lambda", [1, 2, D])
    qg_d = din("q_norm_g", [1, HD])
    kg_d = din("k_norm_g", [1, HD])
    wo_d = din("w_out", [1, D, D])
    n2_d = din("norm2_g", [1, D])
    w1_d = din("w_ffn_in", [1, D, 2 * D_FF])
    w2_d = din("w_ffn_out", [1, D_FF, D])
    cos_d = din("rope_cos", [T, HD])
    sin_d = din("rope_sin", [T, HD])
    out_d = nc.dram_tensor("out", [NB_CORE, SEQ, D], F32, kind="ExternalOutput").ap()
    hbuf_d = (nc.dram_tensor("hbuf", [NB_CORE * SEQ, D], F32, kind="ExternalOutput").ap() if debug
              else nc.dram_tensor("hbuf", [NB_CORE * SEQ, D], F32).ap())
    dbg = {}
    if debug:
        for nm, shp in (("d_xnT", [128, 8 * (T + 4)]), ("d_KT", [128, 2 * T]), ("d_V", [128, 17 * 256]),
                        ("d_mixr", [128, 8 * SEQ])):
            dbg[nm] = nc.dram_tensor(nm, shp, F32, kind="ExternalOutput").ap()

    win_v = win_d[0].rearrange("(k p) n -> p k n", p=128)
    wo_v = wo_d[0].rearrange("(k p) n -> p k n", p=128)
    w1_v = w1_d[0].rearrange("(k p) n -> p k n", p=128)
    w2_v = w2_d[0].rearrange("(k p) n -> p k n", p=128)

    ARENA_W = 53184
    arena_cm = nc.sbuf_tensor("arena", [128, ARENA_W], F32)
    psum_cm = nc.psum_tensor("ps", [128, 8, 512], F32)
    arena = arena_cm.__enter__()
    psum = psum_cm.__enter__()

    class Alloc:
        def __init__(self):
            self.ptr = 0
            self.marks = []

        def f32(self, shape, name):
            n = int(np.prod(shape[1:]))
            v = arena[:, self.ptr:self.ptr + n]
            self.ptr += n
            assert self.ptr <= ARENA_W, (name, self.ptr)
            if len(shape) == 3:
                v = v.rearrange("p (a b) -> p a b", a=shape[1])
            return v

        def bf16(self, shape, name):
            n = int(np.prod(shape[1:]))
            assert n % 2 == 0
            v = arena[:, self.ptr:self.ptr + n // 2].bitcast(BF16)
            self.ptr += n // 2
            assert self.ptr <= ARENA_W, (name, self.ptr)
            if len(shape) == 3:
                v = v.rearrange("p (a b) -> p a b", a=shape[1])
            return v

        def mark(self):
            self.marks.append(self.ptr)

        def release(self):
            self.ptr = self.marks.pop()

    A = Alloc()

    def psf(b, n=512):
        return psum[:, b, 0:n]

    def psb(b):
        return psum[:, b, 0:256].bitcast(BF16).rearrange("p (a b) -> p a b", a=4)

    PB = [Buf("psum%d" % i) for i in range(8)]

    ident = A.bf16([128, 128], "ident")
    ones = A.bf16([128, 128], "ones")
    g1b = A.f32([128, D], "g1b")
    gq = A.f32([128, HD], "gq")
    gqs = A.f32([128, HD], "gqs")
    gk = A.f32([128, HD], "gk")
    gks = A.f32([128, HD], "gks")
    cw = A.f32([128, 4, 8], "cw")
    cb = A.f32([128, 8], "cb")
    bah = A.f32([128, 16], "bah")
    bxh = A.f32([128, 16], "bxh")
    lam = A.f32([128, 16], "lam")
    cc = A.f32([128, 16], "cc")
    cch = A.f32([128, 16], "cch")
    tmpc = A.f32([128, 16], "tmpc")
    tmpe = A.f32([128, 16], "tmpe")
    rstd2_all = A.f32([128, NB_CORE * NT_TILES], "rstd2_all")
    rstd1_all = A.f32([128, 32], "rstd1_all")
    B_rstd1 = [Buf("rstd1_%d" % i) for i in range(17)]
    wab = A.bf16([128, 16 * 128], "wab").rearrange("p (a b) -> p a b", a=16)
    wxb = A.bf16([128, 16 * 128], "wxb").rearrange("p (a b) -> p a b", a=16)
    B_const = Buf("const")
    B_rstd2 = [Buf("rstd2_%d" % i) for i in range(NB_CORE * NT_TILES)]
    B_hbuf = [Buf("hbuf_%d" % i) for i in range(NB_CORE * NT_TILES)]

    def ld(eng, out, in_, wbufs, rbufs=(), **kw):
        return S.op(eng, lambda e, out=out, in_=in_, kw=kw: e.dma_start(out=out, in_=in_, **kw), reads=rbufs, writes=wbufs,
                    dma=True)

    def bc(ap, n):
        return bass.AP(ap.tensor, ap.offset, [[0, 128], [1, n]])

    S.op("pool", lambda e: e.memset(ones, 1.0), writes=[B_const])
    ident_d = din("ident", [128, 128])
    identf = A.f32([128, 128], "identf")
    ld("sp", identf, ident_d, [B_const])
    S.op("dve", lambda e: e.tensor_copy(out=ident, in_=identf), reads=[B_const], writes=[B_const])
    ld("sp", g1b, bc(n1_d[0:1, :], D), [B_const])
    ld("sp", gq, bc(qg_d[0:1, :], HD), [B_const])
    ld("sp", gk, bc(kg_d[0:1, :], HD), [B_const])
    for a in range(2):
        for h in range(2):
            o0 = 64 * a + 32 * h
            s0 = 64 * a + 32 * (1 - h)
            ld("sp", gqs[:, o0:o0 + 32], bc(qg_d[0:1, s0:s0 + 32], 32), [B_const])
            ld("sp", gks[:, o0:o0 + 32], bc(kg_d[0:1, s0:s0 + 32], 32), [B_const])
    ld("sp", cw, cw_d[0].rearrange("j (c p) -> p j c", p=128), [B_const], allow_slow_non_contiguous=True)
    ld("sp", cb, cb_d[0].rearrange("(c p) -> p c", p=128), [B_const], allow_slow_non_contiguous=True)
    ld("sp", bah.rearrange("p (r c) -> p r c", r=2), ba_d[0].rearrange("r (c p) -> p r c", p=128), [B_const],
       allow_slow_non_contiguous=True)
    ld("sp", bxh.rearrange("p (r c) -> p r c", r=2), bx_d[0].rearrange("r (c p) -> p r c", p=128), [B_const],
       allow_slow_non_contiguous=True)
    ld("sp", lam.rearrange("p (r c) -> p r c", r=2), lam_d[0].rearrange("r (c p) -> p r c", p=128), [B_const],
       allow_slow_non_contiguous=True)
    ld("pool", wab.rearrange("p (r n) d -> p r n d", r=2), wa_d[0].rearrange("r n c d -> c r n d"), [B_const])
    ld("pool", wxb.rearrange("p (r n) d -> p r n d", r=2), wx_d[0].rearrange("r n c d -> c r n d"), [B_const])
    S.op("dve", lambda e: e.tensor_scalar(out=bah, in0=bah, scalar1=0.5, scalar2=None, op0=ALU.mult),
         reads=[B_const], writes=[B_const])
    S.op("dve", lambda e: e.tensor_scalar(out=bxh, in0=bxh, scalar1=0.5, scalar2=None, op0=ALU.mult),
         reads=[B_const], writes=[B_const])
    S.op("act", lambda e: e.activation(out=tmpe, in_=lam, func=AF.Exp, scale=-1.0), reads=[B_const], writes=[B_const])
    S.op("dve", lambda e: e.tensor_scalar(out=tmpc, in0=tmpe, scalar1=-0.25, scalar2=1.0 / 3.0, op0=ALU.mult,
                                          op1=ALU.add), reads=[B_const], writes=[B_const])
    S.op("dve", lambda e: e.tensor_tensor(out=tmpc, in0=tmpc, in1=tmpe, op=ALU.mult), reads=[B_const], writes=[B_const])
    S.op("dve", lambda e: e.scalar_tensor_tensor(out=tmpc, in0=tmpc, scalar=-0.5, in1=tmpe, op0=ALU.add, op1=ALU.mult),
         reads=[B_const], writes=[B_const])
    S.op("dve", lambda e: e.scalar_tensor_tensor(out=tmpc, in0=tmpc, scalar=1.0, in1=tmpe, op0=ALU.add, op1=ALU.mult),
         reads=[B_const], writes=[B_const])
    S.op("dve", lambda e: e.tensor_scalar(out=cc, in0=tmpc, scalar1=-8.0, scalar2=None, op0=ALU.mult),
         reads=[B_const], writes=[B_const])
    S.op("dve", lambda e: e.tensor_scalar(out=cch, in0=tmpc, scalar1=-4.0, scalar2=None, op0=ALU.mult),
         reads=[B_const], writes=[B_const])

    A.mark()
    wq = A.bf16([128, 8, 1024], "wq")
    wga = A.bf16([128, 8, 1024], "wga")
    wo = A.bf16([128, 8, 1024], "wo")
    B_wq, B_wga, B_wo = Buf("wq"), Buf("wga"), Buf("wo")
    for k in range(8):
        ld("pool", wq[:, k, :], win_v[:, k, C_Q:C_Q + 1024], [B_wq])
    for k in range(8):
        ld("pool", wga[:, k, :], win_v[:, k, C_GA:C_GA + 1024], [B_wga])
    for k in range(8):
        ld("pool", wo[:, k, :], wo_v[:, k, :], [B_wo])

    XNT0 = A.ptr
    xnT = A.bf16([128, 8, T + 4], "xnT")
    XNT1 = A.ptr
    mixr = A.bf16([128, 8, SEQ], "mixr")
    B_xnT = [Buf("xnT%d" % i) for i in range(17)]
    B_xpad = Buf("xnTpad")
    B_mixr = [[Buf("mixr%d_%d" % (c, i)) for i in range(NT_TILES)] for c in range(8)]
    B_KT = [Buf("KT%d" % i) for i in range(17)]
    B_V = [Buf("V%d" % i) for i in range(17)]

    def tile_cols(i):
        return (0, 16) if i == 0 else (16 + (i - 1) * 128, 128)

    def tiles_overlapping(p0, p1):
        res = []
        for i in range(17):
            s, n = tile_cols(i)
            if s < p1 and s + n > p0:
                res.append(i)
        return res

    WORK0 = A.ptr

    def rsqrt_small(ms, out, n, width, Bms, Bout):
        S.op("act", lambda e: e.activation(out=ms[:n, :width], in_=ms[:n, :width], func=AF.Ln), reads=[Bms], writes=[Bms])
        S.op("act", lambda e: e.activation(out=out[:n, :width], in_=ms[:n, :width], func=AF.Exp, scale=-0.5),
             reads=[Bms], writes=[Bout])

    class TileWork:
        pass

    def alloc_tilework(nx=2, with_q=True):
        W = TileWork()
        W.xt = [A.f32([128, D], "xt") for _ in range(nx)]
        W.Bxt = [Buf("xt%d" % i) for i in range(nx)]
        W.junk = A.bf16([128, D], "junk")
        W.Bjunk = Buf("junk")
        W.ss = [A.f32([128, 16], "ss") for _ in range(2)]
        W.Bss = [Buf("ss%d" % i) for i in range(2)]
        W.rs = [A.f32([128, 16], "rs") for _ in range(2)]
        W.Brs = [Buf("rs%d" % i) for i in range(2)]
        W.xn = [A.bf16([128, D], "xn") for _ in range(2)]
        W.Bxn = [Buf("xn%d" % i) for i in range(2)]
        W.cos = [A.f32([128, HD], "cos") for _ in range(2)]
        W.sin = [A.f32([128, HD], "sin") for _ in range(2)]
        W.Btab = [Buf("tab%d" % i) for i in range(2)]
        W.t1 = A.f32([128, HD], "t1")
        W.t2 = A.f32([128, HD], "t2")
        W.Bt12 = Buf("t12")
        nq = 1024 if with_q else 256
        W.qn = A.f32([128, nq], "qn")
        W.Bqn = Buf("qn")
        W.m1 = A.f32([128, nq], "m1")
        W.Bm1 = Buf("m1")
        W.m2 = A.f32([128, nq], "m2")
        W.Bm2 = Buf("m2")
        W.sq, W.Bsq = W.m2, W.Bm2
        W.qr = A.bf16([128, nq], "qr")
        W.Bqr = Buf("qr")
        W.cnt = 0
        return W

    def load_x(W, slot, b, i):
        s, n = tile_cols(i)
        src = meta_d[0:16, :] if i == 0 else x_d[b, (i - 1) * 128:i * 128, :]
        ld("sp", W.xt[slot][:n, :], src, [W.Bxt[slot]])

    def norm1_tile(W, slot, n, dst_xnT, dst_bufs, banks, stats=None, reuse=None):
        r = W.cnt % 2
        W.cnt += 1
        xt, ss, rs, xn = W.xt[slot], W.ss[r], W.rs[r], W.xn[r]
        if reuse is None:
            S.op("act", lambda e: e.activation(out=W.junk[:n, :], in_=xt[:n, :], func=AF.Square, accum_out=ss[:n, 0:1]),
                 reads=[W.Bxt[slot]], writes=[W.Bjunk, W.Bss[r]])
            yield
            S.op("dve", lambda e: e.tensor_scalar(out=ss[:n, 0:1], in0=ss[:n, 0:1], scalar1=1.0 / D, scalar2=EPS,
                                                  op0=ALU.mult, op1=ALU.add), reads=[W.Bss[r]], writes=[W.Bss[r]])
            yield
            if stats is not None:
                rs_ap, rs_buf = stats
                S.op("act", lambda e: e.activation(out=ss[:n, 0:1], in_=ss[:n, 0:1], func=AF.Ln), reads=[W.Bss[r]], writes=[W.Bss[r]])
                S.op("act", lambda e: e.activation(out=rs_ap[:n, :], in_=ss[:n, 0:1], func=AF.Exp, scale=-0.5),
                     reads=[W.Bss[r]], writes=[rs_buf])
            else:
                rs_ap, rs_buf = rs[:, 0:1], W.Brs[r]
                rsqrt_small(ss, rs, n, 1, W.Bss[r], W.Brs[r])
            yield
        else:
            rs_ap, rs_buf = reuse
        S.op("dve", lambda e: e.scalar_tensor_tensor(out=xn[:n, :], in0=xt[:n, :], scalar=rs_ap[:n, :], in1=g1b[:n, :],
                                                     op0=ALU.mult, op1=ALU.mult),
             reads=[W.Bxt[slot], rs_buf, B_const], writes=[W.Bxn[r]])
        yield
        for half in range(2):
            bk = banks[half]
            for k in range(4):
                kk = 4 * half + k
                S.op("pe", lambda e, bk=bk, k=k, kk=kk: e.transpose(out=psb(bk)[:, k, :n], in_=xn[:n, kk * 128:(kk + 1) * 128],
                                                                    identity=ident[:n, :n]),
                     reads=[W.Bxn[r], B_const], writes=[PB[bk]])
            eng = "act" if half == 0 else "dve"
            if eng == "act":
                S.op("act", lambda e, bk=bk, half=half: e.activation(out=dst_xnT[:, 4 * half:4 * half + 4, :n],
                                                                     in_=psb(bk)[:, :, :n], func=AF.Copy),
                     reads=[PB[bk]], writes=dst_bufs)
            else:
                S.op("dve", lambda e, bk=bk, half=half: e.tensor_copy(out=dst_xnT[:, 4 * half:4 * half + 4, :n],
                                                                      in_=psb(bk)[:, :, :n]),
                     reads=[PB[bk]], writes=dst_bufs)
            yield

    def qk_norm_rope(W, n, src_ps, nh, gmain, gswap, pos0, Bsrc):
        w = nh * 128
        r = W.cnt % 2
        W.cnt += 1
        ss, rs = W.ss[r], W.rs[r]
        tb = W.cnt % 2
        ld("sp", W.cos[tb][:n, :], cos_d[pos0:pos0 + n, :], [W.Btab[tb]])
        ld("sp", W.sin[tb][:n, :], sin_d[pos0:pos0 + n, :], [W.Btab[tb]])
        S.op("act", lambda e: e.activation(out=W.sq[:n, :w], in_=src_ps[:n, :w], func=AF.Square), reads=Bsrc, writes=[W.Bsq])
        yield
        S.op("dve", lambda e: e.tensor_reduce(out=ss[:n, 0:nh], in_=W.sq[:n, :w].rearrange("p (h d) -> p h d", h=nh),
                                              axis=AX.X, op=ALU.add), reads=[W.Bsq], writes=[W.Bss[r]])
        yield
        S.op("dve", lambda e: e.tensor_scalar(out=ss[:n, 0:nh], in0=ss[:n, 0:nh], scalar1=1.0 / HD, scalar2=EPS,
                                              op0=ALU.mult, op1=ALU.add), reads=[W.Bss[r]], writes=[W.Bss[r]])
        yield
        rsqrt_small(ss, rs, n, nh, W.Bss[r], W.Brs[r])
        yield
        S.op("dve", lambda e: e.tensor_tensor(out=W.qn[:n, :w].rearrange("p (h d) -> p h d", h=nh),
                                              in0=src_ps[:n, :w].rearrange("p (h d) -> p h d", h=nh),
                                              in1=rs[:n, 0:nh].unsqueeze(2).to_broadcast([n, nh, HD]), op=ALU.mult),
             reads=Bsrc + [W.Brs[r]], writes=[W.Bqn])
        yield
        S.op("dve", lambda e: e.tensor_tensor(out=W.t1[:n, :], in0=W.cos[tb][:n, :], in1=gmain[:n, :], op=ALU.mult),
             reads=[W.Btab[tb], B_const], writes=[W.Bt12])
        S.op("dve", lambda e: e.tensor_tensor(out=W.t2[:n, :], in0=W.sin[tb][:n, :], in1=gswap[:n, :], op=ALU.mult),
             reads=[W.Btab[tb], B_const], writes=[W.Bt12])
        yield
        S.op("dve", lambda e: e.tensor_tensor(out=W.m1[:n, :w].rearrange("p (h d) -> p h d", h=nh),
                                              in0=W.qn[:n, :w].rearrange("p (h d) -> p h d", h=nh),
                                              in1=W.t1[:n, :].unsqueeze(1).to_broadcast([n, nh, HD]), op=ALU.mult),
             reads=[W.Bqn, W.Bt12], writes=[W.Bm1])
        yield
        for hf in range(2):
            def v4(ap, width=w):
                return ap[:n, :width].rearrange("p (h a f d) -> p h a f d", h=nh, a=2, f=2)
            t2v = W.t2[:n, :].rearrange("p (a f d) -> p a f d", a=2, f=2)
            for a in range(2):
                S.op("dve", lambda e, hf=hf, a=a: e.tensor_tensor(
                    out=v4(W.m2)[:, :, a, hf, :], in0=v4(W.qn)[:, :, a, 1 - hf, :],
                    in1=t2v[:, a, hf, :].unsqueeze(1).to_broadcast([n, nh, 32]), op=ALU.mult),
                    reads=[W.Bqn, W.Bt12], writes=[W.Bm2])
            yield
        S.op("dve", lambda e: e.tensor_tensor(out=W.qr[:n, :w], in0=W.m1[:n, :w], in1=W.m2[:n, :w], op=ALU.add),
             reads=[W.Bm1, W.Bm2], writes=[W.Bqr])
        yield

    def drain(g):
        for _ in g:
            pass

    def dump(name, ap2d, bufs):
        ld("pool", dbg[name], ap2d, [Buf("dump")], rbufs=bufs)

    def rev(ap):
        return bass.AP(ap.tensor, ap.offset + T - 1, [list(ap.ap[0]), [-1, T]])

    for b in range(nb):
        A.ptr = WORK0
        W = alloc_tilework(nx=3, with_q=False)
        S.op("pool", lambda e: e.memset(xnT[:, :, 0:2], 0.0), writes=[B_xpad])
        S.op("pool", lambda e: e.memset(xnT[:, :, T + 2:T + 4], 0.0), writes=[B_xpad])
        load_x(W, 0, b, 0)
        load_x(W, 1, b, 1)
        for i in range(17):
            if i + 2 < 17:
                load_x(W, (i + 2) % 3, b, i + 2)
            s, n = tile_cols(i)
            drain(norm1_tile(W, i % 3, n, xnT[:, :, 2 + s:2 + s + n], [B_xnT[i]], (6, 7),
                             stats=(rstd1_all[:, i:i + 1], B_rstd1[i])))
        if debug and b == 0:
            dump("d_xnT", xnT.rearrange("p a b -> p (a b)"), B_xnT + [B_xpad])
        S.barrier()

        A.ptr = WORK0
        xc = A.f32([128, T], "xc")
        xcb = A.bf16([128, T], "xcb")
        av = A.f32([128, T], "av")
        a2 = A.f32([128, T], "a2")
        hf = [A.f32([128, T], "hf%d" % i) for i in range(2)]
        hb = A.f32([128, T], "hb")
        B_xc, B_xcb, B_hb, B_a, B_a2 = Buf("xc"), Buf("xcb"), Buf("hb"), Buf("a"), Buf("a2")
        B_hf = [Buf("hf0"), Buf("hf1")]
        wst = [A.bf16([128, 8, 384], "wst%d" % i) for i in range(2)]
        B_wst = [Buf("wst0"), Buf("wst1")]
        tmp = [A.f32([128, 512], "tmp%d" % i) for i in range(6)]
        B_tmp = [Buf("tmp%d" % i) for i in range(6)]

        B_wgg = [Buf("wgg0"), Buf("wgg1")]

        def load_wxr(c):
            sl = c % 2
            for k in range(8):
                ld("pool", wst[sl][:, k, 0:128], win_v[:, k, C_XR + c * 128:C_XR + (c + 1) * 128], [B_wst[sl]])

        def load_wgg(c):
            sl = c % 2
            for k in range(8):
                ld("pool", wst[sl][:, k, 128:256], win_v[:, k, C_GR + c * 128:C_GR + (c + 1) * 128], [B_wgg[sl]])
                ld("pool", wst[sl][:, k, 256:384], win_v[:, k, C_GN + c * 128:C_GN + (c + 1) * 128], [B_wgg[sl]])

        def rnn_proj_conv(c):
            sl = c % 2
            for w in range(NWIN):
                p0 = WIN * w
                nout = min(WIN, T - p0)
                nin = nout + 3
                bk = w % 2
                for k in range(8):
                    S.op("pe", lambda e: e.matmul(psf(bk)[:, :nin], lhsT=wst[sl][:, k, 0:128], rhs=xnT[:, k, p0:p0 + nin],
                                                  start=(k == 0), stop=(k == 7)),
                         reads=[B_wst[sl], B_xpad] + [B_xnT[i] for i in tiles_overlapping(p0 - 2, p0 + nout + 1)],
                         writes=[PB[bk]])
                S.op("dve", lambda e: e.tensor_scalar(out=xc[:, p0:p0 + nout], in0=psf(bk)[:, 0:nout], scalar1=cw[:, 0, c:c + 1],
                                                      scalar2=cb[:, c:c + 1], op0=ALU.mult, op1=ALU.add),
                     reads=[PB[bk], B_const], writes=[B_xc])
                yield
                for j in range(1, 4):
                    S.op("dve", lambda e: e.scalar_tensor_tensor(out=xc[:, p0:p0 + nout], in0=psf(bk)[:, j:j + nout], scalar=cw[:, j, c:c + 1],
                                                                 in1=xc[:, p0:p0 + nout], op0=ALU.mult, op1=ALU.add),
                         reads=[PB[bk], B_const, B_xc], writes=[B_xc])
                    yield
            yield "casts"
            for w in range(NWIN):
                p0 = WIN * w
                nout = min(WIN, T - p0)
                S.op("act", lambda e: e.activation(out=xcb[:, p0:p0 + nout], in_=xc[:, p0:p0 + nout], func=AF.Copy),
                     reads=[B_xc], writes=[B_xcb])
            yield

        def rnn_dir(c, d):
            ci = d * 8 + c

            def exps(w):
                p0 = WIN * w
                nout = min(WIN, T - p0)
                tr = w % 2
                S.op("act", lambda e: e.activation(out=av[:, p0:p0 + nout], in_=tmp[tr][:, :nout], func=AF.Exp,
                                                   scale=cch[:, ci:ci + 1], bias=cch[:, ci:ci + 1]),
                     reads=[B_tmp[tr], B_const], writes=[B_a])
                S.op("act", lambda e: e.activation(out=a2[:, p0:p0 + nout], in_=tmp[tr][:, :nout], func=AF.Exp,
                                                   scale=cc[:, ci:ci + 1], bias=cc[:, ci:ci + 1]),
                     reads=[B_tmp[tr], B_const], writes=[B_a2])
            for w in range(NWIN):
                p0 = WIN * w
                nout = min(WIN, T - p0)
                bk = 2 + (w % 2)
                tr = w % 2
                S.op("pe", lambda e: e.matmul(psf(bk)[:, :nout], lhsT=wab[:, ci, :], rhs=xcb[:, p0:p0 + nout], start=True, stop=True),
                     reads=[B_xcb, B_const], writes=[PB[bk]])
                S.op("act", lambda e: e.activation(out=tmp[tr][:, :nout], in_=psf(bk)[:, :nout], func=AF.Tanh, scale=0.5,
                                                   bias=bah[:, ci:ci + 1]), reads=[PB[bk], B_const], writes=[B_tmp[tr]])
                if w > 0:
                    exps(w - 1)
                yield
            exps(NWIN - 1)
            yield
            S.op("act", lambda e: e.activation(out=a2, in_=a2, func=AF.Sqrt, scale=-1.0, bias=1.0), reads=[B_a2], writes=[B_a2])
            S.op("dve", lambda e: e.scalar_tensor_tensor(out=a2, in0=a2, scalar=0.125, in1=xc, op0=ALU.mult, op1=ALU.mult),
                 reads=[B_a2, B_xc], writes=[B_a2])
            yield
            for w in range(NWIN):
                p0 = WIN * w
                nout = min(WIN, T - p0)
                bk = 2 + (w % 2)
                tr = w % 2
                S.op("pe", lambda e: e.matmul(psf(bk)[:, :nout], lhsT=wxb[:, ci, :], rhs=xcb[:, p0:p0 + nout], start=True, stop=True),
                     reads=[B_xcb, B_const], writes=[PB[bk]])
                S.op("act", lambda e: e.activation(out=tmp[tr][:, :nout], in_=psf(bk)[:, :nout], func=AF.Tanh, scale=0.5,
                                                   bias=bxh[:, ci:ci + 1]), reads=[PB[bk], B_const], writes=[B_tmp[tr]])
                S.op("dve", lambda e: e.scalar_tensor_tensor(out=a2[:, p0:p0 + nout], in0=tmp[tr][:, :nout], scalar=1.0,
                                                             in1=a2[:, p0:p0 + nout], op0=ALU.add, op1=ALU.mult),
                     reads=[B_a2, B_tmp[tr]], writes=[B_a2])
                yield
            if d == 0:
                hfc = hf[c % 2]
                S.op("dve", lambda e: e.tensor_tensor_scan(out=hfc, data0=av, data1=a2, initial=0.0, op0=ALU.mult, op1=ALU.add),
                     reads=[B_a, B_a2], writes=[B_hf[c % 2]])
            else:
                S.op("dve", lambda e: e.tensor_tensor_scan(out=rev(hb), data0=rev(av), data1=rev(a2), initial=0.0,
                                                           op0=ALU.mult, op1=ALU.add),
                     reads=[B_a, B_a2], writes=[B_hb])
            yield

        def rnn_combine(c):
            sl = c % 2
            hfc, Bhf = hf[c % 2], B_hf[c % 2]
            for w in range(NWIN):
                p0 = max(WIN * w, N_META)
                p1 = min(WIN * (w + 1), T)
                nout = p1 - p0
                par = w % 2
                b4, b5 = 4 + 2 * par, 5 + 2 * par
                g4, g5, Bg4, Bg5 = tmp[2 + 2 * par], tmp[3 + 2 * par], B_tmp[2 + 2 * par], B_tmp[3 + 2 * par]
                tl = list(range((p0 - N_META) // 128, (p1 - 1 - N_META) // 128 + 1))
                xb = [B_xnT[i] for i in tiles_overlapping(p0, p1)]
                for k in range(8):
                    S.op("pe", lambda e: e.matmul(psf(b4)[:, :nout], lhsT=wst[sl][:, k, 128:256], rhs=xnT[:, k, 2 + p0:2 + p0 + nout],
                                                  start=(k == 0), stop=(k == 7)), reads=[B_wgg[sl]] + xb, writes=[PB[b4]])
                for k in range(8):
                    S.op("pe", lambda e: e.matmul(psf(b5)[:, :nout], lhsT=wst[sl][:, k, 256:384], rhs=xnT[:, k, 2 + p0:2 + p0 + nout],
                                                  start=(k == 0), stop=(k == 7)), reads=[B_wgg[sl]] + xb, writes=[PB[b5]])
                S.op("act", lambda e: e.activation(out=g4[:, :nout], in_=psf(b4)[:, :nout], func=AF.Square), reads=[PB[b4]], writes=[Bg4])
                S.op("act", lambda e: e.activation(out=g5[:, :nout], in_=psf(b5)[:, :nout], func=AF.Tanh, scale=0.5),
                     reads=[PB[b5]], writes=[Bg5])
                S.op("dve", lambda e: e.tensor_scalar(out=g4[:, :nout], in0=g4[:, :nout], scalar1=0.044715 * GK, scalar2=GK,
                                                      op0=ALU.mult, op1=ALU.add), reads=[Bg4], writes=[Bg4])
                yield
                S.op("dve", lambda e: e.tensor_tensor(out=g4[:, :nout], in0=g4[:, :nout], in1=psf(b4)[:, :nout], op=ALU.mult),
                     reads=[Bg4, PB[b4]], writes=[Bg4])
                yield
                S.op("act", lambda e: e.activation(out=g4[:, :nout], in_=g4[:, :nout], func=AF.Tanh), reads=[Bg4], writes=[Bg4])
                S.op("dve", lambda e: e.tensor_tensor(out=hb[:, p0:p0 + nout], in0=hfc[:, p0:p0 + nout], in1=hb[:, p0:p0 + nout], op=ALU.add),
                     reads=[Bhf, B_hb], writes=[B_hb])
                yield
                S.op("dve", lambda e: e.scalar_tensor_tensor(out=g4[:, :nout], in0=g4[:, :nout], scalar=1.0, in1=psf(b4)[:, :nout],
                                                             op0=ALU.add, op1=ALU.mult), reads=[Bg4, PB[b4]], writes=[Bg4])
                yield
                S.op("dve", lambda e: e.tensor_tensor(out=g4[:, :nout], in0=g4[:, :nout], in1=hb[:, p0:p0 + nout], op=ALU.mult),
                     reads=[Bg4, B_hb], writes=[Bg4])
                yield
                S.op("dve", lambda e: e.scalar_tensor_tensor(out=mixr[:, c, p0 - N_META:p0 - N_META + nout], in0=g5[:, :nout], scalar=1.0,
                                                             in1=g4[:, :nout], op0=ALU.add, op1=ALU.mult),
                     reads=[Bg4, Bg5], writes=[B_mixr[c][i] for i in tl])
                yield

        def merge(ga, gb, ratio):
            for _ in ga:
                for _ in range(ratio):
                    next(gb, None)
            drain(gb)

        load_wxr(0)
        load_wgg(0)
        drain(rnn_proj_conv(0))
        for c in range(8):
            if c + 1 < 8:
                load_wxr(c + 1)
            merge(rnn_dir(c, 0), rnn_combine(c - 1) if c > 0 else iter(()), 3)
            if c + 1 < 8:
                load_wgg(c + 1)
            gd = rnn_dir(c, 1)
            gs = rnn_proj_conv(c + 1) if c + 1 < 8 else iter(())
            hold, step = False, 0
            for _ in gd:
                step += 1
                if step >= 7 and not hold:
                    for _ in range(4):
                        if next(gs, "end") in ("casts", "end"):
                            hold = True
                            break
            drain(gs)
        drain(rnn_combine(7))
        if debug and b == 0:
            dump("d_mixr", mixr.rearrange("p a b -> p (a b)"), [x for row in B_mixr for x in row])
        S.barrier()

        A.ptr = WORK0
        KT = A.bf16([128, 2, T], "KT")
        V = A.bf16([128, 17, 256], "V")
        wkv = A.bf16([128, 8, 512], "wkv")
        B_wkv = Buf("wkv")
        for k in range(8):
            ld("pool", wkv[:, k, :], win_v[:, k, C_K:C_K + 512], [B_wkv])
        P3_BASE = A.ptr
        W = alloc_tilework(nx=3, with_q=True)
        sqf = A.f32([128, D], "sqf")
        B_sqf = Buf("sqf")
        for i in range(17):
            s, n = tile_cols(i)
            for k in range(8):
                S.op("pe", lambda e, k=k, s=s, n=n: e.matmul(psf(4)[:n, :], lhsT=xnT[:, k, 2 + s:2 + s + n], rhs=wkv[:, k, :],
                                                             start=(k == 0), stop=(k == 7)),
                     reads=[B_xnT[i], B_wkv], writes=[PB[4]])
            S.op("act", lambda e, i=i, n=n: e.activation(out=V[:n, i, :], in_=psf(4)[:n, 256:512], func=AF.Copy),
                 reads=[PB[4]], writes=[B_V[i]])
            drain(qk_norm_rope(W, n, psf(4), 2, gk, gks, s, [PB[4]]))
            for h in range(2):
                S.op("pe", lambda e, h=h, n=n: e.transpose(out=psb(5)[:, h, :n], in_=W.qr[:n, h * 128:(h + 1) * 128],
                                                           identity=ident[:n, :n]),
                     reads=[W.Bqr, B_const], writes=[PB[5]])
            S.op("act", lambda e, s=s, n=n: e.activation(out=KT[:, :, s:s + n], in_=psb(5)[:, 0:2, :n], func=AF.Copy),
                 reads=[PB[5]], writes=[B_KT[i]])
        if debug and b == 0:
            dump("d_KT", KT.rearrange("p a b -> p (a b)"), B_KT)
            dump("d_V", V.rearrange("p a b -> p (a b)"), B_V)

        ht = [A.f32([128, D], "ht%d" % i) for i in range(2)]
        B_ht = [Buf("ht0"), Buf("ht1")]
        S.barrier()
        save_ptr = A.ptr
        A.ptr = XNT0
        xnTb = [A.bf16([128, 8, 128], "xnTb%d" % i) for i in range(2)]
        B_xnTb = [Buf("xnTb0"), Buf("xnTb1")]
        QTb = [A.bf16([128, 8, 128], "QTb%d" % i) for i in range(2)]
        B_QTb = [Buf("QTb0"), Buf("QTb1")]
        eg = [A.f32([128, 8, 128], "eg%d" % i) for i in range(2)]
        B_eg = [Buf("eg0"), Buf("eg1")]
        PT = [A.bf16([128, 512], "PT%d" % i) for i in range(4)]
        B_PT = [Buf("PT%d" % i) for i in range(4)]
        rr = [A.f32([128, 512], "rr%d" % i) for i in range(2)]
        B_rr = [Buf("rr0"), Buf("rr1")]
        mixb = [A.bf16([128, 8, 128], "mixb%d" % i) for i in range(2)]
        B_mixb = [Buf("mixb0"), Buf("mixb1")]
        assert A.ptr <= XNT1, (A.ptr, XNT1)
        A.ptr = save_ptr
        pt_cnt = [0]

        def prep(qb):
            sl = qb % 2
            pos0 = N_META + qb * 128
            yield from norm1_tile(W, qb % 3, 128, xnTb[sl], [B_xnTb[sl]], (6, 7),
                                  reuse=(rstd1_all[:, qb + 1:qb + 2], B_rstd1[qb + 1]))
            for half in range(2):
                for k in range(8):
                    S.op("pe", lambda e: e.matmul(psf(4 + half), lhsT=xnTb[sl][:, k, :], rhs=wq[:, k, half * 512:(half + 1) * 512],
                                                  start=(k == 0), stop=(k == 7)),
                         reads=[B_xnTb[sl], B_wq], writes=[PB[4 + half]])
                yield
            yield from qk_norm_rope(W, 128, psum[:, 4:6, :].rearrange("p a b -> p (a b)"), 8, gq, gqs, pos0, [PB[4], PB[5]])
            for half in range(2):
                bk = 6 + half
                for k in range(4):
                    hh = 4 * half + k
                    S.op("pe", lambda e: e.transpose(out=psb(bk)[:, k, :], in_=W.qr[:, hh * 128:(hh + 1) * 128], identity=ident),
                         reads=[W.Bqr, B_const], writes=[PB[bk]])
                if half == 0:
                    S.op("act", lambda e: e.activation(out=QTb[sl][:, 0:4, :], in_=psb(bk), func=AF.Copy),
                         reads=[PB[bk]], writes=[B_QTb[sl]])
                else:
                    S.op("dve", lambda e: e.tensor_copy(out=QTb[sl][:, 4:8, :], in_=psb(bk)), reads=[PB[bk]], writes=[B_QTb[sl]])
                yield
            for half in range(2):
                bk = 6 + half
                for hh in range(4):
                    h = 4 * half + hh
                    for k in range(8):
                        S.op("pe", lambda e: e.matmul(psf(bk)[:, hh * 128:(hh + 1) * 128], lhsT=wga[:, k, h * 128:(h + 1) * 128],
                                                      rhs=xnTb[sl][:, k, :], start=(k == 0), stop=(k == 7)),
                             reads=[B_xnTb[sl], B_wga], writes=[PB[bk]])
                    if hh % 2 == 1:
                        yield
                S.op("act", lambda e: e.activation(out=eg[sl][:, 4 * half:4 * half + 4, :].rearrange("p a b -> p (a b)"),
                                                   in_=psf(bk), func=AF.Exp, scale=-1.0),
                     reads=[PB[bk]], writes=[B_eg[sl]])
                yield

        def post(qb):
            sl = qb % 2
            gi = b * NT_TILES + qb
            for half in range(2):
                for c in range(8):
                    S.op("pe", lambda e: e.matmul(psf(4 + half), lhsT=mixb[sl][:, c, :], rhs=wo[:, c, half * 512:(half + 1) * 512],
                                                  start=(c == 0), stop=(c == 7)),
                         reads=[B_mixb[sl], B_wo], writes=[PB[4 + half]])
                yield
            S.op("dve", lambda e: e.tensor_tensor(out=ht[sl], in0=psum[:, 4:6, :].rearrange("p a b -> p (a b)"), in1=W.xt[qb % 3], op=ALU.add),
                 reads=[PB[4], PB[5], W.Bxt[qb % 3]], writes=[B_ht[sl]])
            yield
            ld("sp", hbuf_d[gi * 128:(gi + 1) * 128, :], ht[sl], [B_hbuf[gi]], rbufs=[B_ht[sl]])
            r2 = W.cnt % 2
            W.cnt += 1
            S.op("pool", lambda e: e.tensor_tensor(out=sqf, in0=ht[sl], in1=ht[sl], op=ALU.mult), reads=[B_ht[sl]], writes=[B_sqf])
            yield
            S.op("dve", lambda e: e.tensor_reduce(out=W.ss[r2][:, 0:1], in_=sqf, axis=AX.X, op=ALU.add), reads=[B_sqf], writes=[W.Bss[r2]])
            yield
            S.op("dve", lambda e: e.tensor_scalar(out=W.ss[r2][:, 0:1], in0=W.ss[r2][:, 0:1], scalar1=1.0 / D, scalar2=EPS,
                                                  op0=ALU.mult, op1=ALU.add), reads=[W.Bss[r2]], writes=[W.Bss[r2]])
            yield
            S.op("act", lambda e: e.activation(out=W.ss[r2][:, 0:1], in_=W.ss[r2][:, 0:1], func=AF.Ln), reads=[W.Bss[r2]], writes=[W.Bss[r2]])
            S.op("act", lambda e: e.activation(out=rstd2_all[:, gi:gi + 1], in_=W.ss[r2][:, 0:1], func=AF.Exp, scale=-0.5),
                 reads=[W.Bss[r2]], writes=[B_rstd2[gi]])
            yield

        def chain(*gens):
            for g in gens:
                if g is not None:
                    yield from g

        load_x(W, 0, b, 1)
        drain(prep(0))
        load_x(W, 1, b, 2)
        for qb in range(NT_TILES):
            sl = qb % 2
            nxt = chain(post(qb - 1) if qb > 0 else None, prep(qb + 1) if qb + 1 < NT_TILES else None)
            for j in range(NKV):
                qg_ap = QTb[sl][:, 4 * j:4 * j + 4, :].rearrange("p a b -> p (a b)")

                def s_mm(kc, j=j, qg_ap=qg_ap):
                    ks, nk = tile_cols(kc)
                    sb_ = kc % 2
                    S.op("pe", lambda e: e.matmul(psf(sb_)[:nk, :], lhsT=KT[:, j, ks:ks + nk], rhs=qg_ap, start=True, stop=True),
                         reads=[B_KT[kc], B_QTb[sl]], writes=[PB[sb_]])

                s_mm(0)
                for kc in range(17):
                    ks, nk = tile_cols(kc)
                    sb_ = kc % 2
                    pi = pt_cnt[0] % 4
                    pt_cnt[0] += 1
                    if kc + 1 < 17:
                        s_mm(kc + 1)
                    S.op("act", lambda e: e.activation(out=PT[pi][:nk, :], in_=psf(sb_)[:nk, :], func=AF.Exp, scale=SM_SCALE),
                         reads=[PB[sb_]], writes=[B_PT[pi]])
                    S.op("pe", lambda e: e.matmul(psf(2), lhsT=V[:nk, kc, j * 128:(j + 1) * 128], rhs=PT[pi][:nk, :],
                                                  start=(kc == 0), stop=(kc == 16)),
                         reads=[B_V[kc], B_PT[pi]], writes=[PB[2]])
                    S.op("pe", lambda e: e.matmul(psf(3), lhsT=ones[:nk, :], rhs=PT[pi][:nk, :],
                                                  start=(kc == 0), stop=(kc == 16)),
                         reads=[B_const, B_PT[pi]], writes=[PB[3]])
                    if nxt is not None:
                        next(nxt, None)
                r = j % 2
                eg_ap = eg[sl][:, 4 * j:4 * j + 4, :].rearrange("p a b -> p (a b)")
                S.op("dve", lambda e, r=r, eg_ap=eg_ap: e.scalar_tensor_tensor(out=rr[r], in0=eg_ap, scalar=1.0, in1=psf(3),
                                                                               op0=ALU.add, op1=ALU.mult),
                     reads=[B_eg[sl], PB[3]], writes=[B_rr[r]])
                S.op("dve", lambda e, r=r: e.reciprocal(out=rr[r], in_=rr[r]), reads=[B_rr[r]], writes=[B_rr[r]])
                S.op("dve", lambda e, r=r: e.tensor_tensor(out=rr[r], in0=psf(2), in1=rr[r], op=ALU.mult),
                     reads=[B_rr[r], PB[2]], writes=[B_rr[r]])
                S.op("dve", lambda e, r=r, j=j: e.tensor_tensor(
                    out=mixb[sl][:, 4 * j:4 * j + 4, :], in0=rr[r].rearrange("p (a b) -> p a b", a=4),
                    in1=mixr[:, 4 * j:4 * j + 4, qb * 128:(qb + 1) * 128], op=ALU.add),
                    reads=[B_rr[r]] + [B_mixr[c][qb] for c in range(4 * j, 4 * j + 4)], writes=[B_mixb[sl]])
            drain(nxt)
            if qb + 2 < NT_TILES:
                load_x(W, (qb + 2) % 3, b, qb + 3)
        drain(post(NT_TILES - 1))
        S.barrier()

    A.release()
    A.mark()
    S.barrier()
    g2b = A.f32([128, D], "g2b")
    ld("sp", g2b, bc(n2_d[0:1, :], D), [B_const])
    w1 = A.bf16([128, 8, 2 * D_FF], "w1")
    w2 = A.bf16([128, NFF, D], "w2")
    B_w1 = [Buf("w1_%d" % j) for j in range(NFF)]
    B_w2 = [Buf("w2_%d" % j) for j in range(NFF)]
    for k in range(8):
        ld("pool", w1[:, k, :], w1_v[:, k, :], B_w1)
    for j in range(NFF):
        ld("pool", w2[:, j, :], w2_v[:, j, :], [B_w2[j]])
    GT = 2
    hB = [A.f32([128, GT, D], "hB%d" % i) for i in range(2)]
    B_hB = [[Buf("hB%d_%d" % (i, t)) for t in range(GT)] for i in range(2)]
    hn = [A.bf16([128, D], "hn%d" % i) for i in range(2)]
    B_hn = [Buf("hn0"), Buf("hn1")]
    hnT = [A.bf16([128, 8, GT * 128], "hnT%d" % i) for i in range(2)]
    B_hnT = [Buf("hnT0"), Buf("hnT1")]
    actT = A.bf16([128, NFF, GT * 128], "actT")
    B_act = [Buf("act%d" % j) for j in range(NFF)]
    sg = [A.f32([128, GT * 128], "sg%d" % i) for i in range(3)]
    B_sg = [Buf("sg%d" % i) for i in range(3)]
    n_groups = nb * NT_TILES // GT
    out_flat = out_d.rearrange("b s d -> (b s) d")
    out_ops = []

    def load_h(g):
        for t in range(GT):
            gi = g * GT + t
            ld("sp", hB[g % 2][:, t, :], hbuf_d[gi * 128:(gi + 1) * 128, :], [B_hB[g % 2][t]], rbufs=[B_hbuf[gi]])

    load_h(0)
    for g in range(n_groups):
        sl = g % 2
        if g + 1 < n_groups:
            load_h(g + 1)
        for t in range(GT):
            gi = g * GT + t
            r = t % 2
            S.op("dve", lambda e, t=t, gi=gi, r=r: e.scalar_tensor_tensor(out=hn[r], in0=hB[sl][:, t, :], scalar=rstd2_all[:, gi:gi + 1],
                                                                          in1=g2b, op0=ALU.mult, op1=ALU.mult),
                 reads=[B_hB[sl][t], B_rstd2[gi], B_const], writes=[B_hn[r]])
            for half in range(2):
                bk = 6 + half
                for k in range(4):
                    kk = 4 * half + k
                    S.op("pe", lambda e, bk=bk, k=k, kk=kk, r=r: e.transpose(out=psb(bk)[:, k, :], in_=hn[r][:, kk * 128:(kk + 1) * 128],
                                                                             identity=ident), reads=[B_hn[r], B_const], writes=[PB[bk]])
                if half == 0:
                    S.op("act", lambda e, bk=bk, t=t: e.activation(out=hnT[sl][:, 0:4, t * 128:(t + 1) * 128], in_=psb(bk), func=AF.Copy),
                         reads=[PB[bk]], writes=[B_hnT[sl]])
                else:
                    S.op("dve", lambda e, bk=bk, t=t: e.tensor_copy(out=hnT[sl][:, 4:8, t * 128:(t + 1) * 128], in_=psb(bk)),
                         reads=[PB[bk]], writes=[B_hnT[sl]])
        NTOK = GT * 128
        for j in range(NFF):
            bk = j % 4
            for k in range(8):
                S.op("pe", lambda e, bk=bk, j=j, k=k: e.matmul(psf(bk)[:, 0:NTOK], lhsT=w1[:, k, j * 128:(j + 1) * 128], rhs=hnT[sl][:, k, :],
                                                               start=(k == 0), stop=(k == 7)), reads=[B_w1[j], B_hnT[sl]], writes=[PB[bk]])
            for k in range(8):
                S.op("pe", lambda e, bk=bk, j=j, k=k: e.matmul(psf(bk)[:, NTOK:2 * NTOK], lhsT=w1[:, k, D_FF + j * 128:D_FF + (j + 1) * 128],
                                                               rhs=hnT[sl][:, k, :], start=(k == 0), stop=(k == 7)),
                     reads=[B_w1[j], B_hnT[sl]], writes=[PB[bk]])
            si = j % 3
            S.op("act", lambda e, bk=bk, si=si: e.activation(out=sg[si], in_=psf(bk)[:, 0:NTOK], func=AF.Silu),
                 reads=[PB[bk]], writes=[B_sg[si]])
            S.op("dve", lambda e, bk=bk, si=si, j=j: e.tensor_tensor(out=actT[:, j, :], in0=sg[si], in1=psf(bk)[:, NTOK:2 * NTOK], op=ALU.mult),
                 reads=[B_sg[si], PB[bk]], writes=[B_act[j]])
        for t in range(GT):
            gi = g * GT + t
            for half in range(2):
                bk = 4 + half
                for j in range(NFF):
                    S.op("pe", lambda e, bk=bk, j=j, t=t, half=half: e.matmul(psf(bk), lhsT=actT[:, j, t * 128:(t + 1) * 128],
                                                                               rhs=w2[:, j, half * 512:(half + 1) * 512],
                                                                               start=(j == 0), stop=(j == NFF - 1)),
                         reads=[B_act[j], B_w2[j]], writes=[PB[bk]])
            o = S.op("dve", lambda e, t=t: e.tensor_tensor(out=hB[sl][:, t, :], in0=psum[:, 4:6, :].rearrange("p a b -> p (a b)"),
                                                           in1=hB[sl][:, t, :], op=ALU.add),
                     reads=[PB[4], PB[5], B_hB[sl][t]], writes=[B_hB[sl][t]])
            st = ld("sp", out_flat[gi * 128:(gi + 1) * 128, :], hB[sl][:, t, :], [Buf("ost")], rbufs=[B_hB[sl][t]])
            out_ops.append(st)
    S.op("sp", None, extra=out_ops)
    S.barrier()

    sem_cms = [nc.semaphore("s_" + e) for e in ENGS] + [nc.semaphore("d%d" % i) for i in range(N_DMA_SEMS)]
    sem_objs = [c.__enter__() for c in sem_cms]
    sems = dict(zip(ENGS, sem_objs[:len(ENGS)]))
    dsems = sem_objs[len(ENGS):]
    with nc.Block() as block:
        S.emit(nc, block, sems, dsems)
    for c in reversed(sem_cms):
        c.__exit__(None, None, None)
    psum_cm.__exit__(None, None, None)
    arena_cm.__exit__(None, None, None)
    return nc


_CONST_CACHE = {}


def make_in_maps(inputs, n_cores=N_CORES):
    if "rope" not in _CONST_CACHE:
        _CONST_CACHE["rope"] = rope_tables()
        _CONST_CACHE["ident"] = np.eye(128, dtype=np.float32)
    cos, sin = _CONST_CACHE["rope"]
    x = np.ascontiguousarray(inputs["x"], dtype=np.float32)
    maps = []
    for c in range(n_cores):
        m = {k: np.ascontiguousarray(v, dtype=np.float32) for k, v in inputs.items() if k != "x"}
        m["x"] = np.ascontiguousarray(x[c * NB_CORE:(c + 1) * NB_CORE])
        m["rope_cos"] = cos
        m["rope_sin"] = sin
        m["ident"] = _CONST_CACHE["ident"]
        maps.append(m)
    return maps


def kernel(**inputs):
    nc = build()
    maps = make_in_maps(inputs)
    res = run_bass_kernel_spmd(nc, maps, core_ids=list(range(N_CORES)))
    outs = [np.asarray(r["out"], dtype=np.float32) for r in res.results]
    return np.concatenate(outs, axis=0)
```

```python
import types
import numpy as np
import concourse.bass as bass
import concourse.mybir as mybir
from concourse.bass_utils import run_bass_kernel_spmd

F32 = mybir.dt.float32
BF16 = mybir.dt.bfloat16
ALU = mybir.AluOpType
AF = mybir.ActivationFunctionType
AX = mybir.AxisListType

N_CORES = 8
BATCH, SEQ, D = 32, 2048, 1024
NB_CORE = BATCH // N_CORES
N_META = 16
T = SEQ + N_META
NT_TILES = SEQ // 128
HD = 128
NQ, NKV = 8, 2
D_FF = 2816
NFF = D_FF // 128
IN_W = 5632
EPS = 1e-6
C_Q, C_K, C_V, C_XR, C_GR, C_GA, C_GN = 0, 1024, 1280, 1536, 2560, 3584, 4608
WIN = 509
NWIN = 5
GK = 0.7978845608028654
SM_SCALE = HD ** -0.5


class Buf:
    __slots__ = ("name", "w", "r")

    def __init__(self, name):
        self.name = name
        self.w = None
        self.r = []


class Op:
    __slots__ = ("eng", "fn", "deps", "sig", "sem", "val", "dma")


def _freeze(fn):
    if fn is None or fn.__closure__ is None:
        return fn
    cells = []
    for c in fn.__closure__:
        try:
            cells.append(types.CellType(c.cell_contents))
        except ValueError:
            cells.append(c)
    return types.FunctionType(fn.__code__, fn.__globals__, fn.__name__, fn.__defaults__, tuple(cells))


ENGS = ("pe", "act", "dve", "pool", "sp")
N_DMA_SEMS = 40


class Sched:
    def __init__(self):
        self.ops = {e: [] for e in ENGS}
        self.dma_ops = []

    def op(self, eng, fn, reads=(), writes=(), dma=False, extra=()):
        o = Op()
        o.eng, o.fn, o.dma, o.sig, o.sem, o.val = eng, _freeze(fn), dma, False, None, 0
        deps = list(extra)
        for b in reads:
            if b.w is not None:
                deps.append(b.w)
        for b in writes:
            if b.w is not None:
                deps.append(b.w)
            deps.extend(b.r)
        seen, out = set(), []
        for d in deps:
            if id(d) in seen or d is o:
                continue
            seen.add(id(d))
            if d.eng == "pe" and eng == "pe" and not d.dma:
                continue
            out.append(d)
            d.sig = True
        o.deps = out
        for b in reads:
            b.r.append(o)
        for b in writes:
            b.w = o
            b.r = []
        self.ops[eng].append(o)
        if dma:
            self.dma_ops.append(o)
        return o

    def barrier(self):
        last = []
        for e in ENGS:
            for o in reversed(self.ops[e]):
                if o.fn is not None and not o.dma:
                    last.append(o)
                    break
        dmas = [o for o in self.dma_ops if o.eng == "sp"][-N_DMA_SEMS // 2:] + \
               [o for o in self.dma_ops if o.eng == "pool"][-N_DMA_SEMS // 2:]
        for e in ENGS:
            self.op(e, None, extra=last + dmas)

    def emit(self, nc, block, sems, dsems):
        for e in ENGS:
            cnt = 0
            for o in self.ops[e]:
                if o.dma or not o.sig:
                    continue
                cnt += 1
                o.sem, o.val = sems[e], cnt
        half = len(dsems) // 2
        pools = {"sp": dsems[:half], "pool": dsems[half:]}
        use = {id(x): 0 for x in dsems}
        prev = {id(x): None for x in dsems}
        cnt = {"sp": 0, "pool": 0}
        for o in self.dma_ops:
            pl = pools[o.eng]
            sm = pl[cnt[o.eng] % len(pl)]
            cnt[o.eng] += 1
            use[id(sm)] += 1
            o.sem, o.val = sm, 16 * use[id(sm)]
            if prev[id(sm)] is not None:
                o.deps.append(prev[id(sm)])
            prev[id(sm)] = o

        def run(eng_name, eng):
            waited = {}
            for o in self.ops[eng_name]:
                for d in o.deps:
                    key = id(d.sem)
                    if waited.get(key, 0) < d.val:
                        eng.wait_ge(d.sem, d.val)
                        waited[key] = d.val
                if o.fn is None:
                    continue
                inst = o.fn(eng)
                if o.dma:
                    inst.then_inc(o.sem, 16)
                elif o.sig:
                    inst.then_inc(o.sem, 1)

        @block.tensor
        def _(e):
            run("pe", e)

        @block.scalar
        def _(e):
            run("act", e)

        @block.vector
        def _(e):
            run("dve", e)

        @block.gpsimd
        def _(e):
            run("pool", e)

        @block.sync
        def _(e):
            run("sp", e)


def rope_tables():
    rows = SEQ // 64
    row = np.repeat(np.arange(rows, dtype=np.float32), 64)
    col = np.tile(np.arange(64, dtype=np.float32), rows)
    z = np.zeros((N_META,), np.float32)
    row = np.concatenate([z, row])
    col = np.concatenate([z, col])
    inv = np.exp(-np.log(np.float32(10000.0)) * np.arange(32, dtype=np.float32) / np.float32(32)).astype(np.float32)
    ar = (row[:, None] * inv[None, :]).astype(np.float32)
    ac = (col[:, None] * inv[None, :]).astype(np.float32)
    cos = np.concatenate([np.cos(ar), np.cos(ar), np.cos(ac), np.cos(ac)], axis=1).astype(np.float32)
    sin = np.concatenate([-np.sin(ar), np.sin(ar), -np.sin(ac), np.sin(ac)], axis=1).astype(np.float32)
    return np.ascontiguousarray(cos), np.ascontiguousarray(sin)


def build(nb=NB_CORE, debug=False):
    nc = bass.Bass("TRN2", target_bir_lowering=False)
    S = Sched()

    def din(name, shape):
        return nc.dram_tensor(name, list(shape), F32, kind="ExternalInput").ap()

    x_d = din("x", [NB_CORE, SEQ, D])
    meta_d = din("meta_tokens", [N_META, D])
    n1_d = din("norm1_g", [1, D])
    win_d = din("w_in", [1, D, IN_W])
    cw_d = din("conv_w", [1, 4, D])
    cb_d = din("conv_b", [1, D])
    wa_d = din("rg_wa", [1, 2, 8, 128, 128])
    ba_d = din("rg_ba", [1, 2, D])
    wx_d = din("rg_wx", [1, 2, 8, 128, 128])
    bx_d = din("rg_bx", [1, 2, D])
    lam_d = din("rg_lambda", [1, 2, D])
    qg_d = din("q_norm_g", [1, HD])
    kg_d = din("k_norm_g", [1, HD])
    wo_d = din("w_out", [1, D, D])
    n2_d = din("norm2_g", [1, D])
    w1_d = din("w_ffn_in", [1, D, 2 * D_FF])
    w2_d = din("w_ffn_out", [1, D_FF, D])
    cos_d = din("rope_cos", [T, HD])
    sin_d = din("rope_sin", [T, HD])
    out_d = nc.dram_tensor("out", [NB_CORE, SEQ, D], F32, kind="ExternalOutput").ap()
    hbuf_d = (nc.dram_tensor("hbuf", [NB_CORE * SEQ, D], F32, kind="ExternalOutput").ap() if debug
              else nc.dram_tensor("hbuf", [NB_CORE * SEQ, D], F32).ap())
    dbg = {}
    if debug:
        for nm, shp in (("d_xnT", [128, 8 * (T + 4)]), ("d_KT", [128, 2 * T]), ("d_V", [128, 17 * 256]),
                        ("d_mixr", [128, 8 * SEQ])):
            dbg[nm] = nc.dram_tensor(nm, shp, F32, kind="ExternalOutput").ap()

    win_v = win_d[0].rearrange("(k p) n -> p k n", p=128)
    wo_v = wo_d[0].rearrange("(k p) n -> p k n", p=128)
    w1_v = w1_d[0].rearrange("(k p) n -> p k n", p=128)
    w2_v = w2_d[0].rearrange("(k p) n -> p k n", p=128)

    ARENA_W = 53184
    arena_cm = nc.sbuf_tensor("arena", [128, ARENA_W], F32)
    psum_cm = nc.psum_tensor("ps", [128, 8, 512], F32)
    arena = arena_cm.__enter__()
    psum = psum_cm.__enter__()

    class Alloc:
        def __init__(self):
            self.ptr = 0
            self.marks = []

        def f32(self, shape, name):
            n = int(np.prod(shape[1:]))
            v = arena[:, self.ptr:self.ptr + n]
            self.ptr += n
            assert self.ptr <= ARENA_W, (name, self.ptr)
            if len(shape) == 3:
                v = v.rearrange("p (a b) -> p a b", a=shape[1])
            return v

        def bf16(self, shape, name):
            n = int(np.prod(shape[1:]))
            assert n % 2 == 0
            v = arena[:, self.ptr:self.ptr + n // 2].bitcast(BF16)
            self.ptr += n // 2
            assert self.ptr <= ARENA_W, (name, self.ptr)
            if len(shape) == 3:
                v = v.rearrange("p (a b) -> p a b", a=shape[1])
            return v

        def mark(self):
            self.marks.append(self.ptr)

        def release(self):
            self.ptr = self.marks.pop()

    A = Alloc()

    def psf(b, n=512):
        return psum[:, b, 0:n]

    def psb(b):
        return psum[:, b, 0:256].bitcast(BF16).rearrange("p (a b) -> p a b", a=4)

    PB = [Buf("psum%d" % i) for i in range(8)]

    ident = A.bf16([128, 128], "ident")
    ones = A.bf16([128, 128], "ones")
    g1b = A.f32([128, D], "g1b")
    gq = A.f32([128, HD], "gq")
    gqs = A.f32([128, HD], "gqs")
    gk = A.f32([128, HD], "gk")
    gks = A.f32([128, HD], "gks")
    cw = A.f32([128, 4, 8], "cw")
    cb = A.f32([128, 8], "cb")
    bah = A.f32([128, 16], "bah")
    bxh = A.f32([128, 16], "bxh")
    lam = A.f32([128, 16], "lam")
    cc = A.f32([128, 16], "cc")
    cch = A.f32([128, 16], "cch")
    tmpc = A.f32([128, 16], "tmpc")
    tmpe = A.f32([128, 16], "tmpe")
    rstd2_all = A.f32([128, NB_CORE * NT_TILES], "rstd2_all")
    rstd1_all = A.f32([128, 32], "rstd1_all")
    B_rstd1 = [Buf("rstd1_%d" % i) for i in range(17)]
    wab = A.bf16([128, 16 * 128], "wab").rearrange("p (a b) -> p a b", a=16)
    wxb = A.bf16([128, 16 * 128], "wxb").rearrange("p (a b) -> p a b", a=16)
    B_const = Buf("const")
    B_rstd2 = [Buf("rstd2_%d" % i) for i in range(NB_CORE * NT_TILES)]
    B_hbuf = [Buf("hbuf_%d" % i) for i in range(NB_CORE * NT_TILES)]

    def ld(eng, out, in_, wbufs, rbufs=(), **kw):
        return S.op(eng, lambda e, out=out, in_=in_, kw=kw: e.dma_start(out=out, in_=in_, **kw), reads=rbufs, writes=wbufs,
                    dma=True)

    def bc(ap, n):
        return bass.AP(ap.tensor, ap.offset, [[0, 128], [1, n]])

    S.op("pool", lambda e: e.memset(ones, 1.0), writes=[B_const])
    ident_d = din("ident", [128, 128])
    identf = A.f32([128, 128], "identf")
    ld("sp", identf, ident_d, [B_const])
    S.op("dve", lambda e: e.tensor_copy(out=ident, in_=identf), reads=[B_const], writes=[B_const])
    ld("sp", g1b, bc(n1_d[0:1, :], D), [B_const])
    ld("sp", gq, bc(qg_d[0:1, :], HD), [B_const])
    ld("sp", gk, bc(kg_d[0:1, :], HD), [B_const])
    for a in range(2):
        for h in range(2):
            o0 = 64 * a + 32 * h
            s0 = 64 * a + 32 * (1 - h)
            ld("sp", gqs[:, o0:o0 + 32], bc(qg_d[0:1, s0:s0 + 32], 32), [B_const])
            ld("sp", gks[:, o0:o0 + 32], bc(kg_d[0:1, s0:s0 + 32], 32), [B_const])
    ld("sp", cw, cw_d[0].rearrange("j (c p) -> p j c", p=128), [B_const], allow_slow_non_contiguous=True)
    ld("sp", cb, cb_d[0].rearrange("(c p) -> p c", p=128), [B_const], allow_slow_non_contiguous=True)
    ld("sp", bah.rearrange("p (r c) -> p r c", r=2), ba_d[0].rearrange("r (c p) -> p r c", p=128), [B_const],
       allow_slow_non_contiguous=True)
    ld("sp", bxh.rearrange("p (r c) -> p r c", r=2), bx_d[0].rearrange("r (c p) -> p r c", p=128), [B_const],
       allow_slow_non_contiguous=True)
    ld("sp", lam.rearrange("p (r c) -> p r c", r=2), lam_d[0].rearrange("r (c p) -> p r c", p=128), [B_const],
       allow_slow_non_contiguous=True)
    ld("pool", wab.rearrange("p (r n) d -> p r n d", r=2), wa_d[0].rearrange("r n c d -> c r n d"), [B_const])
    ld("pool", wxb.rearrange("p (r n) d -> p r n d", r=2), wx_d[0].rearrange("r n c d -> c r n d"), [B_const])
    S.op("dve", lambda e: e.tensor_scalar(out=bah, in0=bah, scalar1=0.5, scalar2=None, op0=ALU.mult),
         reads=[B_const], writes=[B_const])
    S.op("dve", lambda e: e.tensor_scalar(out=bxh, in0=bxh, scalar1=0.5, scalar2=None, op0=ALU.mult),
         reads=[B_const], writes=[B_const])
    S.op("act", lambda e: e.activation(out=tmpe, in_=lam, func=AF.Exp, scale=-1.0), reads=[B_const], writes=[B_const])
    S.op("dve", lambda e: e.tensor_scalar(out=tmpc, in0=tmpe, scalar1=-0.25, scalar2=1.0 / 3.0, op0=ALU.mult,
                                          op1=ALU.add), reads=[B_const], writes=[B_const])
    S.op("dve", lambda e: e.tensor_tensor(out=tmpc, in0=tmpc, in1=tmpe, op=ALU.mult), reads=[B_const], writes=[B_const])
    S.op("dve", lambda e: e.scalar_tensor_tensor(out=tmpc, in0=tmpc, scalar=-0.5, in1=tmpe, op0=ALU.add, op1=ALU.mult),
         reads=[B_const], writes=[B_const])
    S.op("dve", lambda e: e.scalar_tensor_tensor(out=tmpc, in0=tmpc, scalar=1.0, in1=tmpe, op0=ALU.add, op1=ALU.mult),
         reads=[B_const], writes=[B_const])
    S.op("dve", lambda e: e.tensor_scalar(out=cc, in0=tmpc, scalar1=-8.0, scalar2=None, op0=ALU.mult),
         reads=[B_const], writes=[B_const])
    S.op("dve", lambda e: e.tensor_scalar(out=cch, in0=tmpc, scalar1=-4.0, scalar2=None, op0=ALU.mult),
         reads=[B_const], writes=[B_const])

    A.mark()
    wq = A.bf16([128, 8, 1024], "wq")
    wga = A.bf16([128, 8, 1024], "wga")
    wo = A.bf16([128, 8, 1024], "wo")
    B_wq, B_wga, B_wo = Buf("wq"), Buf("wga"), Buf("wo")
    for k in range(8):
        ld("pool", wq[:, k, :], win_v[:, k, C_Q:C_Q + 1024], [B_wq])
    for k in range(8):
        ld("pool", wga[:, k, :], win_v[:, k, C_GA:C_GA + 1024], [B_wga])
    for k in range(8):
        ld("pool", wo[:, k, :], wo_v[:, k, :], [B_wo])

    XNT0 = A.ptr
    xnT = A.bf16([128, 8, T + 4], "xnT")
    XNT1 = A.ptr
    mixr = A.bf16([128, 8, SEQ], "mixr")
    B_xnT = [Buf("xnT%d" % i) for i in range(17)]
    B_xpad = Buf("xnTpad")
    B_mixr = [[Buf("mixr%d_%d" % (c, i)) for i in range(NT_TILES)] for c in range(8)]
    B_KT = [Buf("KT%d" % i) for i in range(17)]
    B_V = [Buf("V%d" % i) for i in range(17)]

    def tile_cols(i):
        return (0, 16) if i == 0 else (16 + (i - 1) * 128, 128)

    def tiles_overlapping(p0, p1):
        res = []
        for i in range(17):
            s, n = tile_cols(i)
            if s < p1 and s + n > p0:
                res.append(i)
        return res

    WORK0 = A.ptr

    def rsqrt_small(ms, out, n, width, Bms, Bout):
        S.op("act", lambda e: e.activation(out=ms[:n, :width], in_=ms[:n, :width], func=AF.Ln), reads=[Bms], writes=[Bms])
        S.op("act", lambda e: e.activation(out=out[:n, :width], in_=ms[:n, :width], func=AF.Exp, scale=-0.5),
             reads=[Bms], writes=[Bout])

    class TileWork:
        pass

    def alloc_tilework(nx=2, with_q=True):
        W = TileWork()
        W.xt = [A.f32([128, D], "xt") for _ in range(nx)]
        W.Bxt = [Buf("xt%d" % i) for i in range(nx)]
        W.junk = A.bf16([128, D], "junk")
        W.Bjunk = Buf("junk")
        W.ss = [A.f32([128, 16], "ss") for _ in range(2)]
        W.Bss = [Buf("ss%d" % i) for i in range(2)]
        W.rs = [A.f32([128, 16], "rs") for _ in range(2)]
        W.Brs = [Buf("rs%d" % i) for i in range(2)]
        W.xn = [A.bf16([128, D], "xn") for _ in range(2)]
        W.Bxn = [Buf("xn%d" % i) for i in range(2)]
        W.cos = [A.f32([128, HD], "cos") for _ in range(2)]
        W.sin = [A.f32([128, HD], "sin") for _ in range(2)]
        W.Btab = [Buf("tab%d" % i) for i in range(2)]
        W.t1 = A.f32([128, HD], "t1")
        W.t2 = A.f32([128, HD], "t2")
        W.Bt12 = Buf("t12")
        nq = 1024 if with_q else 256
        W.qn = A.f32([128, nq], "qn")
        W.Bqn = Buf("qn")
        W.m1 = A.f32([128, nq], "m1")
        W.Bm1 = Buf("m1")
        W.m2 = A.f32([128, nq], "m2")
        W.Bm2 = Buf("m2")
        W.sq, W.Bsq = W.m2, W.Bm2
        W.qr = A.bf16([128, nq], "qr")
        W.Bqr = Buf("qr")
        W.cnt = 0
        return W

    def load_x(W, slot, b, i):
        s, n = tile_cols(i)
        src = meta_d[0:16, :] if i == 0 else x_d[b, (i - 1) * 128:i * 128, :]
        ld("sp", W.xt[slot][:n, :], src, [W.Bxt[slot]])

    def norm1_tile(W, slot, n, dst_xnT, dst_bufs, banks, stats=None, reuse=None):
        r = W.cnt % 2
        W.cnt += 1
        xt, ss, rs, xn = W.xt[slot], W.ss[r], W.rs[r], W.xn[r]
        if reuse is None:
            S.op("act", lambda e: e.activation(out=W.junk[:n, :], in_=xt[:n, :], func=AF.Square, accum_out=ss[:n, 0:1]),
                 reads=[W.Bxt[slot]], writes=[W.Bjunk, W.Bss[r]])
            yield
            S.op("dve", lambda e: e.tensor_scalar(out=ss[:n, 0:1], in0=ss[:n, 0:1], scalar1=1.0 / D, scalar2=EPS,
                                                  op0=ALU.mult, op1=ALU.add), reads=[W.Bss[r]], writes=[W.Bss[r]])
            yield
            if stats is not None:
                rs_ap, rs_buf = stats
                S.op("act", lambda e: e.activation(out=ss[:n, 0:1], in_=ss[:n, 0:1], func=AF.Ln), reads=[W.Bss[r]], writes=[W.Bss[r]])
                S.op("act", lambda e: e.activation(out=rs_ap[:n, :], in_=ss[:n, 0:1], func=AF.Exp, scale=-0.5),
                     reads=[W.Bss[r]], writes=[rs_buf])
            else:
                rs_ap, rs_buf = rs[:, 0:1], W.Brs[r]
                rsqrt_small(ss, rs, n, 1, W.Bss[r], W.Brs[r])
            yield
        else:
            rs_ap, rs_buf = reuse
        S.op("dve", lambda e: e.scalar_tensor_tensor(out=xn[:n, :], in0=xt[:n, :], scalar=rs_ap[:n, :], in1=g1b[:n, :],
                                                     op0=ALU.mult, op1=ALU.mult),
             reads=[W.Bxt[slot], rs_buf, B_const], writes=[W.Bxn[r]])
        yield
        for half in range(2):
            bk = banks[half]
            for k in range(4):
                kk = 4 * half + k
                S.op("pe", lambda e, bk=bk, k=k, kk=kk: e.transpose(out=psb(bk)[:, k, :n], in_=xn[:n, kk * 128:(kk + 1) * 128],
                                                                    identity=ident[:n, :n]),
                     reads=[W.Bxn[r], B_const], writes=[PB[bk]])
            eng = "act" if half == 0 else "dve"
            if eng == "act":
                S.op("act", lambda e, bk=bk, half=half: e.activation(out=dst_xnT[:, 4 * half:4 * half + 4, :n],
                                                                     in_=psb(bk)[:, :, :n], func=AF.Copy),
                     reads=[PB[bk]], writes=dst_bufs)
            else:
                S.op("dve", lambda e, bk=bk, half=half: e.tensor_copy(out=dst_xnT[:, 4 * half:4 * half + 4, :n],
                                                                      in_=psb(bk)[:, :, :n]),
                     reads=[PB[bk]], writes=dst_bufs)
            yield

    def qk_norm_rope(W, n, src_ps, nh, gmain, gswap, pos0, Bsrc):
        w = nh * 128
        r = W.cnt % 2
        W.cnt += 1
        ss, rs = W.ss[r], W.rs[r]
        tb = W.cnt % 2
        ld("sp", W.cos[tb][:n, :], cos_d[pos0:pos0 + n, :], [W.Btab[tb]])
        ld("sp", W.sin[tb][:n, :], sin_d[pos0:pos0 + n, :], [W.Btab[tb]])
        S.op("act", lambda e: e.activation(out=W.sq[:n, :w], in_=src_ps[:n, :w], func=AF.Square), reads=Bsrc, writes=[W.Bsq])
        yield
        S.op("dve", lambda e: e.tensor_reduce(out=ss[:n, 0:nh], in_=W.sq[:n, :w].rearrange("p (h d) -> p h d", h=nh),
                                              axis=AX.X, op=ALU.add), reads=[W.Bsq], writes=[W.Bss[r]])
        yield
        S.op("dve", lambda e: e.tensor_scalar(out=ss[:n, 0:nh], in0=ss[:n, 0:nh], scalar1=1.0 / HD, scalar2=EPS,
                                              op0=ALU.mult, op1=ALU.add), reads=[W.Bss[r]], writes=[W.Bss[r]])
        yield
        rsqrt_small(ss, rs, n, nh, W.Bss[r], W.Brs[r])
        yield
        S.op("dve", lambda e: e.tensor_tensor(out=W.qn[:n, :w].rearrange("p (h d) -> p h d", h=nh),
                                              in0=src_ps[:n, :w].rearrange("p (h d) -> p h d", h=nh),
                                              in1=rs[:n, 0:nh].unsqueeze(2).to_broadcast([n, nh, HD]), op=ALU.mult),
             reads=Bsrc + [W.Brs[r]], writes=[W.Bqn])
        yield
        S.op("dve", lambda e: e.tensor_tensor(out=W.t1[:n, :], in0=W.cos[tb][:n, :], in1=gmain[:n, :], op=ALU.mult),
             reads=[W.Btab[tb], B_const], writes=[W.Bt12])
        S.op("dve", lambda e: e.tensor_tensor(out=W.t2[:n, :], in0=W.sin[tb][:n, :], in1=gswap[:n, :], op=ALU.mult),
             reads=[W.Btab[tb], B_const], writes=[W.Bt12])
        yield
        S.op("dve", lambda e: e.tensor_tensor(out=W.m1[:n, :w].rearrange("p (h d) -> p h d", h=nh),
                                              in0=W.qn[:n, :w].rearrange("p (h d) -> p h d", h=nh),
                                              in1=W.t1[:n, :].unsqueeze(1).to_broadcast([n, nh, HD]), op=ALU.mult),
             reads=[W.Bqn, W.Bt12], writes=[W.Bm1])
        yield
        for hf in range(2):
            def v4(ap, width=w):
                return ap[:n, :width].rearrange("p (h a f d) -> p h a f d", h=nh, a=2, f=2)
            t2v = W.t2[:n, :].rearrange("p (a f d) -> p a f d", a=2, f=2)
            for a in range(2):
                S.op("dve", lambda e, hf=hf, a=a: e.tensor_tensor(
                    out=v4(W.m2)[:, :, a, hf, :], in0=v4(W.qn)[:, :, a, 1 - hf, :],
                    in1=t2v[:, a, hf, :].unsqueeze(1).to_broadcast([n, nh, 32]), op=ALU.mult),
                    reads=[W.Bqn, W.Bt12], writes=[W.Bm2])
            yield
        S.op("dve", lambda e: e.tensor_tensor(out=W.qr[:n, :w], in0=W.m1[:n, :w], in1=W.m2[:n, :w], op=ALU.add),
             reads=[W.Bm1, W.Bm2], writes=[W.Bqr])
        yield

    def drain(g):
        for _ in g:
            pass

    def dump(name, ap2d, bufs):
        ld("pool", dbg[name], ap2d, [Buf("dump")], rbufs=bufs)

    def rev(ap):
        return bass.AP(ap.tensor, ap.offset + T - 1, [list(ap.ap[0]), [-1, T]])

    for b in range(nb):
        A.ptr = WORK0
        W = alloc_tilework(nx=3, with_q=False)
        S.op("pool", lambda e: e.memset(xnT[:, :, 0:2], 0.0), writes=[B_xpad])
        S.op("pool", lambda e: e.memset(xnT[:, :, T + 2:T + 4], 0.0), writes=[B_xpad])
        load_x(W, 0, b, 0)
        load_x(W, 1, b, 1)
        for i in range(17):
            if i + 2 < 17:
                load_x(W, (i + 2) % 3, b, i + 2)
            s, n = tile_cols(i)
            drain(norm1_tile(W, i % 3, n, xnT[:, :, 2 + s:2 + s + n], [B_xnT[i]], (6, 7),
                             stats=(rstd1_all[:, i:i + 1], B_rstd1[i])))
        if debug and b == 0:
            dump("d_xnT", xnT.rearrange("p a b -> p (a b)"), B_xnT + [B_xpad])
        S.barrier()

        A.ptr = WORK0
        xc = A.f32([128, T], "xc")
        xcb = A.bf16([128, T], "xcb")
        av = A.f32([128, T], "av")
        a2 = A.f32([128, T], "a2")
        hf = [A.f32([128, T], "hf%d" % i) for i in range(2)]
        hb = A.f32([128, T], "hb")
        B_xc, B_xcb, B_hb, B_a, B_a2 = Buf("xc"), Buf("xcb"), Buf("hb"), Buf("a"), Buf("a2")
        B_hf = [Buf("hf0"), Buf("hf1")]
        wst = [A.bf16([128, 8, 384], "wst%d" % i) for i in range(2)]
        B_wst = [Buf("wst0"), Buf("wst1")]
        tmp = [A.f32([128, 512], "tmp%d" % i) for i in range(6)]
        B_tmp = [Buf("tmp%d" % i) for i in range(6)]

        B_wgg = [Buf("wgg0"), Buf("wgg1")]

        def load_wxr(c):
            sl = c % 2
            for k in range(8):
                ld("pool", wst[sl][:, k, 0:128], win_v[:, k, C_XR + c * 128:C_XR + (c + 1) * 128], [B_wst[sl]])

        def load_wgg(c):
            sl = c % 2
            for k in range(8):
                ld("pool", wst[sl][:, k, 128:256], win_v[:, k, C_GR + c * 128:C_GR + (c + 1) * 128], [B_wgg[sl]])
                ld("pool", wst[sl][:, k, 256:384], win_v[:, k, C_GN + c * 128:C_GN + (c + 1) * 128], [B_wgg[sl]])

        def rnn_proj_conv(c):
            sl = c % 2
            for w in range(NWIN):
                p0 = WIN * w
                nout = min(WIN, T - p0)
                nin = nout + 3
                bk = w % 2
                for k in range(8):
                    S.op("pe", lambda e: e.matmul(psf(bk)[:, :nin], lhsT=wst[sl][:, k, 0:128], rhs=xnT[:, k, p0:p0 + nin],
                                                  start=(k == 0), stop=(k == 7)),
                         reads=[B_wst[sl], B_xpad] + [B_xnT[i] for i in tiles_overlapping(p0 - 2, p0 + nout + 1)],
                         writes=[PB[bk]])
                S.op("dve", lambda e: e.tensor_scalar(out=xc[:, p0:p0 + nout], in0=psf(bk)[:, 0:nout], scalar1=cw[:, 0, c:c + 1],
                                                      scalar2=cb[:, c:c + 1], op0=ALU.mult, op1=ALU.add),
                     reads=[PB[bk], B_const], writes=[B_xc])
                yield
                for j in range(1, 4):
                    S.op("dve", lambda e: e.scalar_tensor_tensor(out=xc[:, p0:p0 + nout], in0=psf(bk)[:, j:j + nout], scalar=cw[:, j, c:c + 1],
                                                                 in1=xc[:, p0:p0 + nout], op0=ALU.mult, op1=ALU.add),
                         reads=[PB[bk], B_const, B_xc], writes=[B_xc])
                    yield
            yield "casts"
            for w in range(NWIN):
                p0 = WIN * w
                nout = min(WIN, T - p0)
                S.op("act", lambda e: e.activation(out=xcb[:, p0:p0 + nout], in_=xc[:, p0:p0 + nout], func=AF.Copy),
                     reads=[B_xc], writes=[B_xcb])
            yield

        def rnn_dir(c, d):
            ci = d * 8 + c

            def exps(w):
                p0 = WIN * w
                nout = min(WIN, T - p0)
                tr = w % 2
                S.op("act", lambda e: e.activation(out=av[:, p0:p0 + nout], in_=tmp[tr][:, :nout], func=AF.Exp,
                                                   scale=cch[:, ci:ci + 1], bias=cch[:, ci:ci + 1]),
                     reads=[B_tmp[tr], B_const], writes=[B_a])
                S.op("act", lambda e: e.activation(out=a2[:, p0:p0 + nout], in_=tmp[tr][:, :nout], func=AF.Exp,
                                                   scale=cc[:, ci:ci + 1], bias=cc[:, ci:ci + 1]),
                     reads=[B_tmp[tr], B_const], writes=[B_a2])
            for w in range(NWIN):
                p0 = WIN * w
                nout = min(WIN, T - p0)
                bk = 2 + (w % 2)
                tr = w % 2
                S.op("pe", lambda e: e.matmul(psf(bk)[:, :nout], lhsT=wab[:, ci, :], rhs=xcb[:, p0:p0 + nout], start=True, stop=True),
                     reads=[B_xcb, B_const], writes=[PB[bk]])
                S.op("act", lambda e: e.activation(out=tmp[tr][:, :nout], in_=psf(bk)[:, :nout], func=AF.Tanh, scale=0.5,
                                                   bias=bah[:, ci:ci + 1]), reads=[PB[bk], B_const], writes=[B_tmp[tr]])
                if w > 0:
                    exps(w - 1)
                yield
            exps(NWIN - 1)
            yield
            S.op("act", lambda e: e.activation(out=a2, in_=a2, func=AF.Sqrt, scale=-1.0, bias=1.0), reads=[B_a2], writes=[B_a2])
            S.op("dve", lambda e: e.scalar_tensor_tensor(out=a2, in0=a2, scalar=0.125, in1=xc, op0=ALU.mult, op1=ALU.mult),
                 reads=[B_a2, B_xc], writes=[B_a2])
            yield
            for w in range(NWIN):
                p0 = WIN * w
                nout = min(WIN, T - p0)
                bk = 2 + (w % 2)
                tr = w % 2
                S.op("pe", lambda e: e.matmul(psf(bk)[:, :nout], lhsT=wxb[:, ci, :], rhs=xcb[:, p0:p0 + nout], start=True, stop=True),
                     reads=[B_xcb, B_const], writes=[PB[bk]])
                S.op("act", lambda e: e.activation(out=tmp[tr][:, :nout], in_=psf(bk)[:, :nout], func=AF.Tanh, scale=0.5,
                                                   bias=bxh[:, ci:ci + 1]), reads=[PB[bk], B_const], writes=[B_tmp[tr]])
                S.op("dve", lambda e: e.scalar_tensor_tensor(out=a2[:, p0:p0 + nout], in0=tmp[tr][:, :nout], scalar=1.0,
                                                             in1=a2[:, p0:p0 + nout], op0=ALU.add, op1=ALU.mult),
                     reads=[B_a2, B_tmp[tr]], writes=[B_a2])
                yield
            if d == 0:
                hfc = hf[c % 2]
                S.op("dve", lambda e: e.tensor_tensor_scan(out=hfc, data0=av, data1=a2, initial=0.0, op0=ALU.mult, op1=ALU.add),
                     reads=[B_a, B_a2], writes=[B_hf[c % 2]])
            else:
                S.op("dve", lambda e: e.tensor_tensor_scan(out=rev(hb), data0=rev(av), data1=rev(a2), initial=0.0,
                                                           op0=ALU.mult, op1=ALU.add),
                     reads=[B_a, B_a2], writes=[B_hb])
            yield

        def rnn_combine(c):
            sl = c % 2
            hfc, Bhf = hf[c % 2], B_hf[c % 2]
            for w in range(NWIN):
                p0 = max(WIN * w, N_META)
                p1 = min(WIN * (w + 1), T)
                nout = p1 - p0
                par = w % 2
                b4, b5 = 4 + 2 * par, 5 + 2 * par
                g4, g5, Bg4, Bg5 = tmp[2 + 2 * par], tmp[3 + 2 * par], B_tmp[2 + 2 * par], B_tmp[3 + 2 * par]
                tl = list(range((p0 - N_META) // 128, (p1 - 1 - N_META) // 128 + 1))
                xb = [B_xnT[i] for i in tiles_overlapping(p0, p1)]
                for k in range(8):
                    S.op("pe", lambda e: e.matmul(psf(b4)[:, :nout], lhsT=wst[sl][:, k, 128:256], rhs=xnT[:, k, 2 + p0:2 + p0 + nout],
                                                  start=(k == 0), stop=(k == 7)), reads=[B_wgg[sl]] + xb, writes=[PB[b4]])
                for k in range(8):
                    S.op("pe", lambda e: e.matmul(psf(b5)[:, :nout], lhsT=wst[sl][:, k, 256:384], rhs=xnT[:, k, 2 + p0:2 + p0 + nout],
                                                  start=(k == 0), stop=(k == 7)), reads=[B_wgg[sl]] + xb, writes=[PB[b5]])
                S.op("act", lambda e: e.activation(out=g4[:, :nout], in_=psf(b4)[:, :nout], func=AF.Square), reads=[PB[b4]], writes=[Bg4])
                S.op("act", lambda e: e.activation(out=g5[:, :nout], in_=psf(b5)[:, :nout], func=AF.Tanh, scale=0.5),
                     reads=[PB[b5]], writes=[Bg5])
                S.op("dve", lambda e: e.tensor_scalar(out=g4[:, :nout], in0=g4[:, :nout], scalar1=0.044715 * GK, scalar2=GK,
                                                      op0=ALU.mult, op1=ALU.add), reads=[Bg4], writes=[Bg4])
                yield
                S.op("dve", lambda e: e.tensor_tensor(out=g4[:, :nout], in0=g4[:, :nout], in1=psf(b4)[:, :nout], op=ALU.mult),
                     reads=[Bg4, PB[b4]], writes=[Bg4])
                yield
                S.op("act", lambda e: e.activation(out=g4[:, :nout], in_=g4[:, :nout], func=AF.Tanh), reads=[Bg4], writes=[Bg4])
                S.op("dve", lambda e: e.tensor_tensor(out=hb[:, p0:p0 + nout], in0=hfc[:, p0:p0 + nout], in1=hb[:, p0:p0 + nout], op=ALU.add),
                     reads=[Bhf, B_hb], writes=[B_hb])
                yield
                S.op("dve", lambda e: e.scalar_tensor_tensor(out=g4[:, :nout], in0=g4[:, :nout], scalar=1.0, in1=psf(b4)[:, :nout],
                                                             op0=ALU.add, op1=ALU.mult), reads=[Bg4, PB[b4]], writes=[Bg4])
                yield
                S.op("dve", lambda e: e.tensor_tensor(out=g4[:, :nout], in0=g4[:, :nout], in1=hb[:, p0:p0 + nout], op=ALU.mult),
                     reads=[Bg4, B_hb], writes=[Bg4])
                yield
                S.op("dve", lambda e: e.scalar_tensor_tensor(out=mixr[:, c, p0 - N_META:p0 - N_META + nout], in0=g5[:, :nout], scalar=1.0,
                                                             in1=g4[:, :nout], op0=ALU.add, op1=ALU.mult),
                     reads=[Bg4, Bg5], writes=[B_mixr[c][i] for i in tl])
                yield

        def merge(ga, gb, ratio):
            for _ in ga:
                for _ in range(ratio):
                    next(gb, None)
            drain(gb)

        load_wxr(0)
        load_wgg(0)
        drain(rnn_proj_conv(0))
        for c in range(8):
            if c + 1 < 8:
                load_wxr(c + 1)
            merge(rnn_dir(c, 0), rnn_combine(c - 1) if c > 0 else iter(()), 3)
            if c + 1 < 8:
                load_wgg(c + 1)
            gd = rnn_dir(c, 1)
            gs = rnn_proj_conv(c + 1) if c + 1 < 8 else iter(())
            hold, step = False, 0
            for _ in gd:
                step += 1
                if step >= 7 and not hold:
                    for _ in range(4):
                        if next(gs, "end") in ("casts", "end"):
                            hold = True
                            break
            drain(gs)
        drain(rnn_combine(7))
        if debug and b == 0:
            dump("d_mixr", mixr.rearrange("p a b -> p (a b)"), [x for row in B_mixr for x in row])
        S.barrier()

        A.ptr = WORK0
        KT = A.bf16([128, 2, T], "KT")
        V = A.bf16([128, 17, 256], "V")
        wkv = A.bf16([128, 8, 512], "wkv")
        B_wkv = Buf("wkv")
        for k in range(8):
            ld("pool", wkv[:, k, :], win_v[:, k, C_K:C_K + 512], [B_wkv])
        P3_BASE = A.ptr
        W = alloc_tilework(nx=3, with_q=True)
        sqf = A.f32([128, D], "sqf")
        B_sqf = Buf("sqf")
        for i in range(17):
            s, n = tile_cols(i)
            for k in range(8):
                S.op("pe", lambda e, k=k, s=s, n=n: e.matmul(psf(4)[:n, :], lhsT=xnT[:, k, 2 + s:2 + s + n], rhs=wkv[:, k, :],
                                                             start=(k == 0), stop=(k == 7)),
                     reads=[B_xnT[i], B_wkv], writes=[PB[4]])
            S.op("act", lambda e, i=i, n=n: e.activation(out=V[:n, i, :], in_=psf(4)[:n, 256:512], func=AF.Copy),
                 reads=[PB[4]], writes=[B_V[i]])
            drain(qk_norm_rope(W, n, psf(4), 2, gk, gks, s, [PB[4]]))
            for h in range(2):
                S.op("pe", lambda e, h=h, n=n: e.transpose(out=psb(5)[:, h, :n], in_=W.qr[:n, h * 128:(h + 1) * 128],
                                                           identity=ident[:n, :n]),
                     reads=[W.Bqr, B_const], writes=[PB[5]])
            S.op("act", lambda e, s=s, n=n: e.activation(out=KT[:, :, s:s + n], in_=psb(5)[:, 0:2, :n], func=AF.Copy),
                 reads=[PB[5]], writes=[B_KT[i]])
        if debug and b == 0:
            dump("d_KT", KT.rearrange("p a b -> p (a b)"), B_KT)
            dump("d_V", V.rearrange("p a b -> p (a b)"), B_V)

        ht = [A.f32([128, D], "ht%d" % i) for i in range(2)]
        B_ht = [Buf("ht0"), Buf("ht1")]
        S.barrier()
        save_ptr = A.ptr
        A.ptr = XNT0
        xnTb = [A.bf16([128, 8, 128], "xnTb%d" % i) for i in range(2)]
        B_xnTb = [Buf("xnTb0"), Buf("xnTb1")]
        QTb = [A.bf16([128, 8, 128], "QTb%d" % i) for i in range(2)]
        B_QTb = [Buf("QTb0"), Buf("QTb1")]
        eg = [A.f32([128, 8, 128], "eg%d" % i) for i in range(2)]
        B_eg = [Buf("eg0"), Buf("eg1")]
        PT = [A.bf16([128, 512], "PT%d" % i) for i in range(4)]
        B_PT = [Buf("PT%d" % i) for i in range(4)]
        rr = [A.f32([128, 512], "rr%d" % i) for i in range(2)]
        B_rr = [Buf("rr0"), Buf("rr1")]
        mixb = [A.bf16([128, 8, 128], "mixb%d" % i) for i in range(2)]
        B_mixb = [Buf("mixb0"), Buf("mixb1")]
        assert A.ptr <= XNT1, (A.ptr, XNT1)
        A.ptr = save_ptr
        pt_cnt = [0]

        def prep(qb):
            sl = qb % 2
            pos0 = N_META + qb * 128
            yield from norm1_tile(W, qb % 3, 128, xnTb[sl], [B_xnTb[sl]], (6, 7),
                                  reuse=(rstd1_all[:, qb + 1:qb + 2], B_rstd1[qb + 1]))
            for half in range(2):
                for k in range(8):
                    S.op("pe", lambda e: e.matmul(psf(4 + half), lhsT=xnTb[sl][:, k, :], rhs=wq[:, k, half * 512:(half + 1) * 512],
                                                  start=(k == 0), stop=(k == 7)),
                         reads=[B_xnTb[sl], B_wq], writes=[PB[4 + half]])
                yield
            yield from qk_norm_rope(W, 128, psum[:, 4:6, :].rearrange("p a b -> p (a b)"), 8, gq, gqs, pos0, [PB[4], PB[5]])
            for half in range(2):
                bk = 6 + half
                for k in range(4):
                    hh = 4 * half + k
                    S.op("pe", lambda e: e.transpose(out=psb(bk)[:, k, :], in_=W.qr[:, hh * 128:(hh + 1) * 128], identity=ident),
                         reads=[W.Bqr, B_const], writes=[PB[bk]])
                if half == 0:
                    S.op("act", lambda e: e.activation(out=QTb[sl][:, 0:4, :], in_=psb(bk), func=AF.Copy),
                         reads=[PB[bk]], writes=[B_QTb[sl]])
                else:
                    S.op("dve", lambda e: e.tensor_copy(out=QTb[sl][:, 4:8, :], in_=psb(bk)), reads=[PB[bk]], writes=[B_QTb[sl]])
                yield
            for half in range(2):
                bk = 6 + half
                for hh in range(4):
                    h = 4 * half + hh
                    for k in range(8):
                        S.op("pe", lambda e: e.matmul(psf(bk)[:, hh * 128:(hh + 1) * 128], lhsT=wga[:, k, h * 128:(h + 1) * 128],
                                                      rhs=xnTb[sl][:, k, :], start=(k == 0), stop=(k == 7)),
                             reads=[B_xnTb[sl], B_wga], writes=[PB[bk]])
                    if hh % 2 == 1:
                        yield
                S.op("act", lambda e: e.activation(out=eg[sl][:, 4 * half:4 * half + 4, :].rearrange("p a b -> p (a b)"),
                                                   in_=psf(bk), func=AF.Exp, scale=-1.0),
                     reads=[PB[bk]], writes=[B_eg[sl]])
                yield

        def post(qb):
            sl = qb % 2
            gi = b * NT_TILES + qb
            for half in range(2):
                for c in range(8):
                    S.op("pe", lambda e: e.matmul(psf(4 + half), lhsT=mixb[sl][:, c, :], rhs=wo[:, c, half * 512:(half + 1) * 512],
                                                  start=(c == 0), stop=(c == 7)),
                         reads=[B_mixb[sl], B_wo], writes=[PB[4 + half]])
                yield
            S.op("dve", lambda e: e.tensor_tensor(out=ht[sl], in0=psum[:, 4:6, :].rearrange("p a b -> p (a b)"), in1=W.xt[qb % 3], op=ALU.add),
                 reads=[PB[4], PB[5], W.Bxt[qb % 3]], writes=[B_ht[sl]])
            yield
            ld("sp", hbuf_d[gi * 128:(gi + 1) * 128, :], ht[sl], [B_hbuf[gi]], rbufs=[B_ht[sl]])
            r2 = W.cnt % 2
            W.cnt += 1
            S.op("pool", lambda e: e.tensor_tensor(out=sqf, in0=ht[sl], in1=ht[sl], op=ALU.mult), reads=[B_ht[sl]], writes=[B_sqf])
            yield
            S.op("dve", lambda e: e.tensor_reduce(out=W.ss[r2][:, 0:1], in_=sqf, axis=AX.X, op=ALU.add), reads=[B_sqf], writes=[W.Bss[r2]])
            yield
            S.op("dve", lambda e: e.tensor_scalar(out=W.ss[r2][:, 0:1], in0=W.ss[r2][:, 0:1], scalar1=1.0 / D, scalar2=EPS,
                                                  op0=ALU.mult, op1=ALU.add), reads=[W.Bss[r2]], writes=[W.Bss[r2]])
            yield
            S.op("act", lambda e: e.activation(out=W.ss[r2][:, 0:1], in_=W.ss[r2][:, 0:1], func=AF.Ln), reads=[W.Bss[r2]], writes=[W.Bss[r2]])
            S.op("act", lambda e: e.activation(out=rstd2_all[:, gi:gi + 1], in_=W.ss[r2][:, 0:1], func=AF.Exp, scale=-0.5),
                 reads=[W.Bss[r2]], writes=[B_rstd2[gi]])
            yield

        def chain(*gens):
            for g in gens:
                if g is not None:
                    yield from g

        load_x(W, 0, b, 1)
        drain(prep(0))
        load_x(W, 1, b, 2)
        for qb in range(NT_TILES):
            sl = qb % 2
            nxt = chain(post(qb - 1) if qb > 0 else None, prep(qb + 1) if qb + 1 < NT_TILES else None)
            for j in range(NKV):
                qg_ap = QTb[sl][:, 4 * j:4 * j + 4, :].rearrange("p a b -> p (a b)")

                def s_mm(kc, j=j, qg_ap=qg_ap):
                    ks, nk = tile_cols(kc)
                    sb_ = kc % 2
                    S.op("pe", lambda e: e.matmul(psf(sb_)[:nk, :], lhsT=KT[:, j, ks:ks + nk], rhs=qg_ap, start=True, stop=True),
                         reads=[B_KT[kc], B_QTb[sl]], writes=[PB[sb_]])

                s_mm(0)
                for kc in range(17):
                    ks, nk = tile_cols(kc)
                    sb_ = kc % 2
                    pi = pt_cnt[0] % 4
                    pt_cnt[0] += 1
                    if kc + 1 < 17:
                        s_mm(kc + 1)
                    S.op("act", lambda e: e.activation(out=PT[pi][:nk, :], in_=psf(sb_)[:nk, :], func=AF.Exp, scale=SM_SCALE),
                         reads=[PB[sb_]], writes=[B_PT[pi]])
                    S.op("pe", lambda e: e.matmul(psf(2), lhsT=V[:nk, kc, j * 128:(j + 1) * 128], rhs=PT[pi][:nk, :],
                                                  start=(kc == 0), stop=(kc == 16)),
                         reads=[B_V[kc], B_PT[pi]], writes=[PB[2]])
                    S.op("pe", lambda e: e.matmul(psf(3), lhsT=ones[:nk, :], rhs=PT[pi][:nk, :],
                                                  start=(kc == 0), stop=(kc == 16)),
                         reads=[B_const, B_PT[pi]], writes=[PB[3]])
                    if nxt is not None:
                        next(nxt, None)
                r = j % 2
                eg_ap = eg[sl][:, 4 * j:4 * j + 4, :].rearrange("p a b -> p (a b)")
                S.op("dve", lambda e, r=r, eg_ap=eg_ap: e.scalar_tensor_tensor(out=rr[r], in0=eg_ap, scalar=1.0, in1=psf(3),
                                                                               op0=ALU.add, op1=ALU.mult),
                     reads=[B_eg[sl], PB[3]], writes=[B_rr[r]])
                S.op("dve", lambda e, r=r: e.reciprocal(out=rr[r], in_=rr[r]), reads=[B_rr[r]], writes=[B_rr[r]])
                S.op("dve", lambda e, r=r: e.tensor_tensor(out=rr[r], in0=psf(2), in1=rr[r], op=ALU.mult),
                     reads=[B_rr[r], PB[2]], writes=[B_rr[r]])
                S.op("dve", lambda e, r=r, j=j: e.tensor_tensor(
                    out=mixb[sl][:, 4 * j:4 * j + 4, :], in0=rr[r].rearrange("p (a b) -> p a b", a=4),
                    in1=mixr[:, 4 * j:4 * j + 4, qb * 128:(qb + 1) * 128], op=ALU.add),
                    reads=[B_rr[r]] + [B_mixr[c][qb] for c in range(4 * j, 4 * j + 4)], writes=[B_mixb[sl]])
            drain(nxt)
            if qb + 2 < NT_TILES:
                load_x(W, (qb + 2) % 3, b, qb + 3)
        drain(post(NT_TILES - 1))
        S.barrier()

    A.release()
    A.mark()
    S.barrier()
    g2b = A.f32([128, D], "g2b")
    ld("sp", g2b, bc(n2_d[0:1, :], D), [B_const])
    w1 = A.bf16([128, 8, 2 * D_FF], "w1")
    w2 = A.bf16([128, NFF, D], "w2")
    B_w1 = [Buf("w1_%d" % j) for j in range(NFF)]
    B_w2 = [Buf("w2_%d" % j) for j in range(NFF)]
    for k in range(8):
        ld("pool", w1[:, k, :], w1_v[:, k, :], B_w1)
    for j in range(NFF):
        ld("pool", w2[:, j, :], w2_v[:, j, :], [B_w2[j]])
    GT = 2
    hB = [A.f32([128, GT, D], "hB%d" % i) for i in range(2)]
    B_hB = [[Buf("hB%d_%d" % (i, t)) for t in range(GT)] for i in range(2)]
    hn = [A.bf16([128, D], "hn%d" % i) for i in range(2)]
    B_hn = [Buf("hn0"), Buf("hn1")]
    hnT = [A.bf16([128, 8, GT * 128], "hnT%d" % i) for i in range(2)]
    B_hnT = [Buf("hnT0"), Buf("hnT1")]
    actT = A.bf16([128, NFF, GT * 128], "actT")
    B_act = [Buf("act%d" % j) for j in range(NFF)]
    sg = [A.f32([128, GT * 128], "sg%d" % i) for i in range(3)]
    B_sg = [Buf("sg%d" % i) for i in range(3)]
    n_groups = nb * NT_TILES // GT
    out_flat = out_d.rearrange("b s d -> (b s) d")
    out_ops = []

    def load_h(g):
        for t in range(GT):
            gi = g * GT + t
            ld("sp", hB[g % 2][:, t, :], hbuf_d[gi * 128:(gi + 1) * 128, :], [B_hB[g % 2][t]], rbufs=[B_hbuf[gi]])

    def ffn_prep(g):
        sl = g % 2
        for t in range(GT):
            gi = g * GT + t
            r = t % 2
            S.op("dve", lambda e: e.scalar_tensor_tensor(out=hn[r], in0=hB[sl][:, t, :], scalar=rstd2_all[:, gi:gi + 1],
                                                         in1=g2b, op0=ALU.mult, op1=ALU.mult),
                 reads=[B_hB[sl][t], B_rstd2[gi], B_const], writes=[B_hn[r]])
            yield
            yield
            for half in range(2):
                bk = 6 + half
                for k in range(4):
                    kk = 4 * half + k
                    S.op("pe", lambda e: e.transpose(out=psb(bk)[:, k, :], in_=hn[r][:, kk * 128:(kk + 1) * 128], identity=ident),
                         reads=[B_hn[r], B_const], writes=[PB[bk]])
                if half == 0:
                    S.op("act", lambda e: e.activation(out=hnT[sl][:, 0:4, t * 128:(t + 1) * 128], in_=psb(bk), func=AF.Copy),
                         reads=[PB[bk]], writes=[B_hnT[sl]])
                else:
                    S.op("dve", lambda e: e.tensor_copy(out=hnT[sl][:, 4:8, t * 128:(t + 1) * 128], in_=psb(bk)),
                         reads=[PB[bk]], writes=[B_hnT[sl]])
                yield

    load_h(0)
    drain(ffn_prep(0))
    for g in range(n_groups):
        sl = g % 2
        if g + 1 < n_groups:
            load_h(g + 1)
        nxtp = ffn_prep(g + 1) if g + 1 < n_groups else iter(())
        NTOK = GT * 128
        for j in range(NFF):
            bk = j % 4
            for k in range(8):
                S.op("pe", lambda e, bk=bk, j=j, k=k: e.matmul(psf(bk)[:, 0:NTOK], lhsT=w1[:, k, j * 128:(j + 1) * 128], rhs=hnT[sl][:, k, :],
                                                               start=(k == 0), stop=(k == 7)), reads=[B_w1[j], B_hnT[sl]], writes=[PB[bk]])
            for k in range(8):
                S.op("pe", lambda e, bk=bk, j=j, k=k: e.matmul(psf(bk)[:, NTOK:2 * NTOK], lhsT=w1[:, k, D_FF + j * 128:D_FF + (j + 1) * 128],
                                                               rhs=hnT[sl][:, k, :], start=(k == 0), stop=(k == 7)),
                     reads=[B_w1[j], B_hnT[sl]], writes=[PB[bk]])
            si = j % 3
            S.op("act", lambda e, bk=bk, si=si: e.activation(out=sg[si], in_=psf(bk)[:, 0:NTOK], func=AF.Silu),
                 reads=[PB[bk]], writes=[B_sg[si]])
            S.op("dve", lambda e, bk=bk, si=si, j=j: e.tensor_tensor(out=actT[:, j, :], in0=sg[si], in1=psf(bk)[:, NTOK:2 * NTOK], op=ALU.mult),
                 reads=[B_sg[si], PB[bk]], writes=[B_act[j]])
            if j >= 6:
                next(nxtp, None)
        drain(nxtp)
        for t in range(GT):
            gi = g * GT + t
            for half in range(2):
                bk = 4 + half
                for j in range(NFF):
                    S.op("pe", lambda e, bk=bk, j=j, t=t, half=half: e.matmul(psf(bk), lhsT=actT[:, j, t * 128:(t + 1) * 128],
                                                                               rhs=w2[:, j, half * 512:(half + 1) * 512],
                                                                               start=(j == 0), stop=(j == NFF - 1)),
                         reads=[B_act[j], B_w2[j]], writes=[PB[bk]])
            o = S.op("dve", lambda e, t=t: e.tensor_tensor(out=hB[sl][:, t, :], in0=psum[:, 4:6, :].rearrange("p a b -> p (a b)"),
                                                           in1=hB[sl][:, t, :], op=ALU.add),
                     reads=[PB[4], PB[5], B_hB[sl][t]], writes=[B_hB[sl][t]])
            st = ld("sp", out_flat[gi * 128:(gi + 1) * 128, :], hB[sl][:, t, :], [Buf("ost")], rbufs=[B_hB[sl][t]])
            out_ops.append(st)
    S.op("sp", None, extra=out_ops)
    S.barrier()

    sem_cms = [nc.semaphore("s_" + e) for e in ENGS] + [nc.semaphore("d%d" % i) for i in range(N_DMA_SEMS)]
    sem_objs = [c.__enter__() for c in sem_cms]
    sems = dict(zip(ENGS, sem_objs[:len(ENGS)]))
    dsems = sem_objs[len(ENGS):]
    with nc.Block() as block:
        S.emit(nc, block, sems, dsems)
    for c in reversed(sem_cms):
        c.__exit__(None, None, None)
    psum_cm.__exit__(None, None, None)
    arena_cm.__exit__(None, None, None)
    return nc


_CONST_CACHE = {}


def make_in_maps(inputs, n_cores=N_CORES):
    if "rope" not in _CONST_CACHE:
        _CONST_CACHE["rope"] = rope_tables()
        _CONST_CACHE["ident"] = np.eye(128, dtype=np.float32)
    cos, sin = _CONST_CACHE["rope"]
    x = np.ascontiguousarray(inputs["x"], dtype=np.float32)
    maps = []
    for c in range(n_cores):
        m = {k: np.ascontiguousarray(v, dtype=np.float32) for k, v in inputs.items() if k != "x"}
        m["x"] = np.ascontiguousarray(x[c * NB_CORE:(c + 1) * NB_CORE])
        m["rope_cos"] = cos
        m["rope_sin"] = sin
        m["ident"] = _CONST_CACHE["ident"]
        maps.append(m)
    return maps


def kernel(**inputs):
    nc = build()
    maps = make_in_maps(inputs)
    res = run_bass_kernel_spmd(nc, maps, core_ids=list(range(N_CORES)))
    outs = [np.asarray(r["out"], dtype=np.float32) for r in res.results]
    return np.concatenate(outs, axis=0)
```
